# Optimizing a Trainium2 kernel written in Bass

```python
import math
import jax, jax.numpy as jnp
from jax import lax
import numpy as np

D_MODEL = 1024
BATCH = 8
SEQ = 4096
DEPTH = 1
DEC_BATCH = 1
DEC_SEQ = 16384
PAST_LEN = 128

HEAD_DIM = 64
GRID_W = 64
NA_HEADS = 8
NA_ROWS = 8
NA_COLS = 16
NA_QBLK = 16
NA_KBLK = 32
NA_WIDTH = NA_HEADS * HEAD_DIM
DIL_GROUPS = ((128, 1), (512, 4), (2048, 16))
DIL_HEADS = 8
DIL_WIDTH = DIL_HEADS * HEAD_DIM
ROPE_THETA = 10000.0
EPS = 1e-6
NEG = -1e30
IN_SIZES = ((NA_WIDTH,) * 4
            + (DIL_WIDTH,) * (3 * len(DIL_GROUPS))
            + (DIL_WIDTH,)
            + (D_MODEL, D_MODEL))
D_IN = sum(IN_SIZES)

kernel_name = "hybrid_natten_dilated_encoder"


def _split_points(sizes):
    pts, acc = [], 0
    for s in sizes[:-1]:
        acc += s
        pts.append(acc)
    return pts


def rms_norm(x, g):
    xf = x.astype(jnp.float32)
    y = xf * lax.rsqrt(jnp.mean(xf * xf, axis=-1, keepdims=True) + EPS)
    return (y * g.astype(jnp.float32)).astype(x.dtype)


def rope(x):
    L, dh = x.shape[1], x.shape[-1]
    inv = ROPE_THETA ** (-jnp.arange(0, dh, 2, dtype=jnp.float32) / dh)
    ang = jnp.arange(L, dtype=jnp.float32)[:, None] * inv[None, :]
    cos = jnp.cos(ang)[None, :, None, :]
    sin = jnp.sin(ang)[None, :, None, :]
    xf = x.astype(jnp.float32)
    x1, x2 = xf[..., : dh // 2], xf[..., dh // 2:]
    return jnp.concatenate([x1 * cos - x2 * sin, x2 * cos + x1 * sin], axis=-1).astype(x.dtype)


def neighborhood_attention(q, k, v, rel_bias):
    B, L, H, dh = q.shape
    rows = L // GRID_W
    kr = min(NA_ROWS, rows)
    ncb = GRID_W // NA_QBLK
    r = jnp.arange(rows)
    row_start = jnp.clip(r - kr // 2, 0, rows - kr)
    key_rows = row_start[:, None] + jnp.arange(kr)[None, :]
    qcol = jnp.arange(ncb)[:, None] * NA_QBLK + jnp.arange(NA_QBLK)[None, :]
    kblk_start = jnp.clip(jnp.arange(ncb) * NA_QBLK - NA_COLS // 2, 0, GRID_W - NA_KBLK)
    key_cols = kblk_start[:, None] + jnp.arange(NA_KBLK)[None, :]
    cstart = jnp.clip(qcol - NA_COLS // 2, 0, GRID_W - NA_COLS)
    kc = key_cols[:, None, :]
    col_ok = (kc >= cstart[:, :, None]) & (kc < cstart[:, :, None] + NA_COLS)
    dcol = jnp.clip(kc - qcol[:, :, None], -(NA_COLS - 1), NA_COLS - 1)
    drow = key_rows - r[:, None]
    h_idx = jnp.arange(H)[None, None, None, :, None, None]
    r_idx = (drow + NA_ROWS - 1)[:, None, None, None, :, None]
    c_idx = (dcol + NA_COLS - 1)[None, :, :, None, None, :]
    bias = rel_bias[h_idx, r_idx, c_idx].astype(jnp.float32)
    kg = k.reshape(B, rows, GRID_W, H, dh)
    vg = v.reshape(B, rows, GRID_W, H, dh)
    ridx = key_rows[:, :, None, None]
    cidx = key_cols[None, None, :, :]
    kb = kg[:, ridx, cidx]
    vb = vg[:, ridx, cidx]
    qb = q.reshape(B, rows, ncb, NA_QBLK, H, dh)
    s = jnp.einsum('brnqhd,brknchd->brnqhkc', qb, kb).astype(jnp.float32) / math.sqrt(dh) + bias
    s = jnp.where(col_ok[:, :, None, None, :], s, NEG)
    p = jax.nn.softmax(s.reshape(s.shape[:-2] + (kr * NA_KBLK,)), axis=-1).reshape(s.shape)
    o = jnp.einsum('brnqhkc,brknchd->brnqhd', p.astype(v.dtype), vb)
    return o.reshape(B, L, H, dh)


def band_attention(q, k, v, half):
    Bs, N, H, dh = q.shape
    blk = half
    nb = -(-N // blk)
    n_pad = nb * blk
    qp = jnp.pad(q, ((0, 0), (0, n_pad - N), (0, 0), (0, 0))).reshape(Bs, nb, blk, H, dh)
    pad_kv = ((0, 0), (blk, n_pad - N + blk), (0, 0), (0, 0))
    kp = jnp.pad(k, pad_kv)
    vp = jnp.pad(v, pad_kv)
    idx = jnp.arange(nb)[:, None] * blk + jnp.arange(3 * blk)[None, :]
    kw = kp[:, idx]
    vw = vp[:, idx]
    kpos = idx - blk
    qpos = jnp.arange(nb)[:, None] * blk + jnp.arange(blk)[None, :]
    ok = ((jnp.abs(kpos[:, None, :] - qpos[:, :, None]) <= half)
          & (kpos >= 0)[:, None, :] & (kpos < N)[:, None, :])
    s = jnp.einsum('bnqhd,bnkhd->bnhqk', qp, kw).astype(jnp.float32) / math.sqrt(dh)
    s = jnp.where(ok[:, None], s, NEG)
    m = jnp.max(s, axis=-1, keepdims=True)
    p = jnp.exp(s - m)
    den = jnp.sum(p, axis=-1, keepdims=True)
    o = jnp.einsum('bnhqk,bnkhd->bnqhd', (p / den).astype(v.dtype), vw)
    lse = (m + jnp.log(den))[..., 0].transpose(0, 1, 3, 2)
    return o.reshape(Bs, n_pad, H, dh)[:, :N], lse.reshape(Bs, n_pad, H)[:, :N]


def dilated_attention(q, k, v, window, dil):
    B, L, H, dh = q.shape
    n = L // dil

    def to_sub(t):
        return t.reshape(B, n, dil, H, dh).transpose(0, 2, 1, 3, 4).reshape(B * dil, n, H, dh)

    o, lse = band_attention(to_sub(q), to_sub(k), to_sub(v), (window // dil) // 2)
    o = o.reshape(B, dil, n, H, dh).transpose(0, 2, 1, 3, 4).reshape(B, L, H, dh)
    lse = lse.reshape(B, dil, n, H).transpose(0, 2, 1, 3).reshape(B, L, H)
    return o, lse


def encoder_layer(x, norm_gain, w_in, qn_a, kn_a, rel_bias_a, qn_b, kn_b, w_branch_a, w_branch_b, w_out):
    B, L, _ = x.shape
    h = rms_norm(x, norm_gain)
    proj = jnp.einsum('bld,de->ble', h, w_in)
    parts = jnp.split(proj, _split_points(IN_SIZES), axis=-1)

    def heads(t):
        return t.reshape(B, L, -1, HEAD_DIM)

    q_a = rms_norm(heads(parts[0]), qn_a)
    k_a = rms_norm(heads(parts[1]), kn_a)
    o_a = neighborhood_attention(q_a, k_a, heads(parts[2]), rel_bias_a).reshape(B, L, NA_WIDTH)
    g_a = parts[3]

    outs, lses = [], []
    for gi, (win, dil) in enumerate(DIL_GROUPS):
        base = 4 + 3 * gi
        q = rope(rms_norm(heads(parts[base]), qn_b))
        k = rope(rms_norm(heads(parts[base + 1]), kn_b))
        o, lse = dilated_attention(q, k, heads(parts[base + 2]), win, dil)
        outs.append(o.astype(jnp.float32))
        lses.append(lse)
    wts = jax.nn.softmax(jnp.stack(lses), axis=0)
    o_b = jnp.einsum('gblh,gblhd->blhd', wts, jnp.stack(outs)).astype(x.dtype).reshape(B, L, DIL_WIDTH)
    g_b, m_a, m_b = parts[-3], parts[-2], parts[-1]

    br_a = jnp.einsum('blc,cd->bld', o_a * jax.nn.silu(g_a), w_branch_a)
    br_b = jnp.einsum('blc,cd->bld', o_b * jax.nn.silu(g_b), w_branch_b)
    merged = jax.nn.sigmoid(m_a) * br_a + jax.nn.sigmoid(m_b) * br_b
    return x + jnp.einsum('bld,de->ble', merged, w_out)


def setup_inputs(seed: int = 0) -> dict:
    key = jax.random.key(seed)
    ks = jax.random.split(key, 13)
    f32 = jnp.float32
    nrm = lambda k, shape, scale: jax.random.normal(k, shape, f32) * scale
    return {
        "x_prompt": nrm(ks[0], (BATCH, SEQ, D_MODEL), 1.0),
        "x_sample": nrm(ks[1], (DEC_BATCH, DEC_SEQ, D_MODEL), 1.0),
        "norm_gain": 1.0 + nrm(ks[2], (DEPTH, D_MODEL), 0.05),
        "w_in": nrm(ks[3], (DEPTH, D_MODEL, D_IN), D_MODEL ** -0.5),
        "qn_a": 1.0 + nrm(ks[4], (DEPTH, HEAD_DIM), 0.05),
        "kn_a": 1.0 + nrm(ks[5], (DEPTH, HEAD_DIM), 0.05),
        "rel_bias_a": nrm(ks[6], (DEPTH, NA_HEADS, 2 * NA_ROWS - 1, 2 * NA_COLS - 1), 0.1),
        "qn_b": 1.0 + nrm(ks[7], (DEPTH, HEAD_DIM), 0.05),
        "kn_b": 1.0 + nrm(ks[8], (DEPTH, HEAD_DIM), 0.05),
        "w_branch_a": nrm(ks[9], (DEPTH, NA_WIDTH, D_MODEL), NA_WIDTH ** -0.5),
        "w_branch_b": nrm(ks[10], (DEPTH, DIL_WIDTH, D_MODEL), DIL_WIDTH ** -0.5),
        "w_out": nrm(ks[11], (DEPTH, D_MODEL, D_MODEL), D_MODEL ** -0.5),
    }


def reference(x_prompt, x_sample, norm_gain, w_in, qn_a, kn_a, rel_bias_a, qn_b, kn_b, w_branch_a, w_branch_b, w_out):
    y_prompt = x_prompt
    y_sample = x_sample
    for l in range(DEPTH):
        y_prompt = encoder_layer(y_prompt, norm_gain[l], w_in[l], qn_a[l], kn_a[l], rel_bias_a[l],
                                 qn_b[l], kn_b[l], w_branch_a[l], w_branch_b[l], w_out[l])
        y_sample = encoder_layer(y_sample, norm_gain[l], w_in[l], qn_a[l], kn_a[l], rel_bias_a[l],
                                 qn_b[l], kn_b[l], w_branch_a[l], w_branch_b[l], w_out[l])
    return (y_prompt, y_sample)
```

```python
import numpy as np
import concourse.bass as bass
import concourse.mybir as mybir
from concourse.bass_utils import run_bass_kernel_spmd

F32 = mybir.dt.float32
BF16 = mybir.dt.bfloat16
AF = mybir.ActivationFunctionType
ALU = mybir.AluOpType

NCORES = 8
NPHYS = 8
MAX_SWDGE_INFLIGHT = 3
LAT = 300.0
FIN_ENG = "dve"
ROPE_ADD_POOL = 0
MERGE_ADD_POOL = 0
NSC = 2
WINDOW = 256
DO_SCHED = True
D = 1024
DIN = 9216
W = 4096
Q0 = 1024
NQ = 2048
NCH = 3
EPS = 1e-6
QA, KA, VA, GA = 0, 512, 1024, 1536
DIL = ((1, 2048, 2560, 3072), (4, 3584, 4096, 4608), (16, 5120, 5632, 6144))
GB, MA, MB = 6656, 7168, 8192
NVF = 20 + 17 + 20 + 32
VF_OFF = {"na": 0, 1: 20, 4: 37, 16: 57}


class Atom:
    __slots__ = ("lw", "rd", "vb")

    def __init__(self):
        self.lw = None
        self.rd = {}
        self.vb = None


class VB:
    __slots__ = ("atom", "ops", "phys", "nsched", "lastfin")

    def __init__(self):
        self.atom = Atom()
        self.atom.vb = self
        self.ops = []
        self.phys = None
        self.nsched = 0
        self.lastfin = 0.0


class LazyAP:
    __slots__ = ("vb", "off", "dims", "p0", "pn", "shape", "bf")

    def __init__(self, vb, off, dims, p0, pn, bf=None):
        self.vb = vb
        self.off = off
        self.dims = dims
        self.p0 = p0
        self.pn = pn
        self.bf = bf
        self.shape = (pn,) + tuple(c for _, c in dims)


class Op:
    __slots__ = ("eng", "fn", "deps", "odeps", "dma", "needed", "sig", "idx", "cost", "fin", "gi", "vbs")

    def __init__(self, eng, fn, dma):
        self.eng = eng
        self.fn = fn
        self.dma = dma
        self.deps = []
        self.odeps = []
        self.needed = False
        self.sig = None
        self.cost = 300.0
        self.fin = None
        self.vbs = []


class Sched:
    ENGS = ("pe", "act", "dve", "pool", "sp")

    def __init__(self):
        self.ops = {e: [] for e in self.ENGS}
        self.atoms = {}
        self.dma_keys = {}
        self.gcount = 0
        self.vbs_all = []

    def A(self, *key):
        a = self.atoms.get(key)
        if a is None:
            a = self.atoms[key] = Atom()
        return a

    def R(self, name, a, b, g=128):
        return [self.A(name, i) for i in range(a // g, (b - 1) // g + 1)]

    def emit(self, eng, fn, reads=(), writes=(), dma=None, cost=300.0):
        op = Op(eng, fn, dma)
        op.idx = len(self.ops[eng])
        op.cost = cost
        op.gi = self.gcount
        self.gcount += 1
        deps = {}
        odeps = {}

        def add(p):
            if p is None:
                return
            if p.eng == "pe" and eng == "pe" and p.dma is None:
                odeps[id(p)] = p
                return
            deps[id(p)] = p

        for b in reads:
            add(b.lw)
        for b in writes:
            add(b.lw)
            for p in b.rd.values():
                add(p)
        op.deps = list(deps.values())
        op.odeps = list(odeps.values())
        for b in list(reads) + list(writes):
            if b.vb is not None and b.vb not in op.vbs:
                op.vbs.append(b.vb)
                b.vb.ops.append(op)
        rk = ("e", id(op))
        for b in reads:
            b.rd[rk] = op
        for b in writes:
            b.lw = op
            b.rd = {}
        self.ops[eng].append(op)
        if dma is not None:
            self.dma_keys.setdefault(dma, 0)
        return op

    def schedule(self, window=48, lat=LAT, nphys=7):
        rem = {e: list(self.ops[e]) for e in self.ENGS}
        out = {e: [] for e in self.ENGS}
        free = {e: 0.0 for e in self.ENGS}
        total = sum(len(v) for v in rem.values())
        done = 0
        rdy = {}
        blk = {}
        issue = {"sp": 100.0, "pool": 1200.0}
        occ = [None] * nphys

        vbs_all = self.vbs_all
        vptr = [0]
        RESERVE = 2

        def oldest_vb():
            i = vptr[0]
            while i < len(vbs_all) and vbs_all[i].phys is not None:
                i += 1
            vptr[0] = i
            return vbs_all[i] if i < len(vbs_all) else None

        def pick_phys(vb):
            bp = None
            bt = None
            nfree = 0
            for p in range(nphys):
                u = occ[p]
                if u is None:
                    t = 0.0
                elif u.nsched == len(u.ops):
                    t = u.lastfin
                else:
                    continue
                nfree += 1
                if bt is None or t < bt:
                    bt = t
                    bp = p
            if bp is not None and nfree < 1 + RESERVE and vb is not oldest_vb():
                return None, None
            return bp, bt

        while done < total:
            best = None
            for e in self.ENGS:
                lst = rem[e]
                if not lst:
                    continue
                cand = None
                fe = free[e]
                for k in range(min(window, len(lst))):
                    op = lst[k]
                    oid = id(op)
                    ready = rdy.get(oid)
                    if ready is None:
                        b_ = blk.get(oid)
                        if b_ is not None and b_.fin is None:
                            continue
                        ready = 0.0
                        ok = True
                        for p in op.deps:
                            if p.fin is None:
                                blk[oid] = p
                                ok = False
                                break
                            t = p.fin + lat
                            if t > ready:
                                ready = t
                        if ok:
                            for p in op.odeps:
                                if p.fin is None:
                                    blk[oid] = p
                                    ok = False
                                    break
                                if p.fin > ready:
                                    ready = p.fin
                        if not ok:
                            continue
                        rdy[oid] = ready
                    newvb = False
                    for vb in op.vbs:
                        if vb.phys is None:
                            newvb = True
                            bp, bt = pick_phys(vb)
                            if bp is None:
                                ready = None
                            elif bt + lat > ready:
                                ready = bt + lat
                            break
                    if ready is None:
                        continue
                    st = ready if ready > fe else fe
                    if cand is None or st < cand[0] - 1e-9:
                        cand = (st, k, op)
                    if st <= fe:
                        break
                if cand is not None and (best is None or cand[0] < best[0][0]):
                    best = (cand, e)
            (st, k, op), e = best
            rem[e].pop(k)
            out[e].append(op)
            for vb in op.vbs:
                if vb.phys is None:
                    bp, bt = pick_phys(vb)
                    u = occ[bp]
                    if u is not None:
                        last = {}
                        for q in u.ops:
                            kk = q.eng if q.dma is None else ("d", id(q))
                            if kk not in last or q.fin > last[kk].fin:
                                last[kk] = q
                        for q in last.values():
                            if q.eng == "pe" and op.eng == "pe" and q.dma is None:
                                op.odeps.append(q)
                            else:
                                op.deps.append(q)
                    occ[bp] = vb
                    vb.phys = bp
            op.fin = st + op.cost
            for vb in op.vbs:
                vb.nsched += 1
                if op.fin > vb.lastfin:
                    vb.lastfin = op.fin
            if op.dma is not None:
                free[e] = st + issue.get(e, 100.0)
            else:
                free[e] = op.fin
            done += 1
        self.ops = out
        self.est_ns = max(o.fin for e in self.ENGS for o in out[e])
        return self.est_ns


def build_program():
    nc = bass.Bass("TRN2", target_bir_lowering=False)
    S = Sched()

    def din(name, shape):
        return nc.dram_tensor(name, list(shape), F32, kind="ExternalInput").ap()

    xw = din("xw", (NCH, W, D))
    w_in = din("w_in", (D, DIN))
    wba = din("wba", (512, D))
    wbb = din("wbb", (512, D))
    wout = din("wout", (D, D))
    ngd = din("ng", (128, 8))
    gcd = din("gcols", (128, 4))
    ropec = din("ropec", (NCH, 128, W))
    ropes = din("ropes", (NCH, 128, W))
    natab = din("natab", (4, 4, 128, 1920))
    vfld = din("vfl", (NCH, 128, NVF))
    cmat = din("cmat", (3, 128, 128))
    bmask = din("bmask", (128, 512))
    yo = nc.dram_tensor("yo", [NCH, NQ, D], F32, kind="ExternalOutput").ap()

    import contextlib
    es = contextlib.ExitStack()

    def sb(name, cols, dt):
        return es.enter_context(nc.sbuf_tensor(name, [128, cols], dt))

    def psb(name, cols, dt):
        return es.enter_context(nc.psum_tensor(name, [128, cols], dt))

    with es:
        hT = sb("hT", 8 * W, BF16)
        og = sb("og", 2 * 4 * NQ, BF16)
        NSLOT = 6
        Wr = sb("Wr", NSLOT * 1024, BF16)
        AR = sb("AR", 16384, BF16)
        RT = sb("RT", 8192, BF16)
        NAT = sb("NAT", 3840, BF16)
        XA = sb("XA", 4096, F32)
        PT = sb("PT", 4 * 512, BF16)
        sq = sb("sq", NSC * 512, BF16)
        rs = sb("rs", NSC * 512, F32)
        qn = sb("qn", NSC * 512, BF16)
        t1 = sb("t1", NSC * 512, F32)
        t2 = sb("t2", NSC * 512, F32)
        xn = sb("xn", 2 * 1024, BF16)
        st_ = sb("stats", 16, F32)
        ng = sb("ngs", 8, F32)
        gc = sb("gcs", 4, F32)
        cm = sb("cm", 3 * 128, BF16)
        bm = sb("bm", 512, BF16)
        vfl = sb("vfls", NVF, F32)

        psT = psb("psT", 512, F32)
        psA = [psb("psA%d" % i, 512, F32) for i in range(2)]
        psX = [psb("psX%d" % i, 512, F32) for i in range(2)]
        psS = [psb("psS%d" % i, 512, F32) for i in range(2)]
        psO = psb("psO", 512, F32)
        psO0 = psO

        FREE = {"hT": 8 * W, "og": 2 * 4 * NQ, "Wr": NSLOT * 1024, "AR": 16384, "RT": 8192, "XA": 4096, "NAT": 3840,
                "PT": 2048, "sq": NSC * 512, "rs": NSC * 512, "qn": NSC * 512, "t1": NSC * 512, "t2": NSC * 512, "junk": 1024,
                "xn": 2048, "stats": 16, "ngs": 8, "gcs": 4, "cm": 384, "bm": 512, "vfls": NVF,
                "psT": 1024, "psO": 512}
        for i in range(2):
            FREE["psA%d" % i] = 512
            FREE["psX%d" % i] = 512
            FREE["psS%d" % i] = 512
        TN = {}

        def V(t, off, dims, p0=0, pn=128):
            if isinstance(t, VB):
                return LazyAP(t, off, dims, p0, pn)
            Fz = FREE[t.name]
            return bass.AP(t, p0 * Fz + off, [[Fz, pn]] + [[s, c] for s, c in dims])

        def RS(x):
            if isinstance(x, LazyAP):
                t = banks[x.vb.phys]
                if x.bf is not None:
                    vw = t[:, :].bitcast(BF16)
                    if x.bf[0] == "cols":
                        return vw[:, x.bf[1]:x.bf[2]]
                    return vw[:, 0:1024].rearrange("p (c t) -> p c t", c=8)
                return bass.AP(t, x.p0 * 512 + x.off, [[512, x.pn]] + [[s_, c_] for s_, c_ in x.dims])
            return x

        def VT(vb, a, b):
            return LazyAP(vb, 0, [(1, b - a)], 0, 128, bf=("cols", a, b))

        def VT8(vb):
            return LazyAP(vb, 0, [(128, 8), (1, 128)], 0, 128, bf=("c8",))

        def nfree(ap):
            n = 1
            for d_ in ap.shape[1:]:
                n *= d_
            return n

        def _is_fast(ap):
            return (not isinstance(ap, LazyAP)) and ap.dtype == BF16 and ap.ap[-1][0] == 1

        def _strided(ap):
            return (not isinstance(ap, LazyAP)) and ap.ap[-1][0] not in (0, 1)

        def vcost(eng, out, *ins):
            n = nfree(out)
            if eng != "dve":
                return 150.0 + 1.9 * n
            aps = [a_ for a_ in (out,) + ins if hasattr(a_, "shape")]
            if all(_is_fast(a_) for a_ in aps):
                return 120.0 + 0.55 * n
            if any(_strided(a_) for a_ in aps):
                return 150.0 + 1.9 * n
            return 100.0 + 1.15 * n

        def mm(out, lhsT, rhs, start, stop, reads, writes):
            S.emit("pe", lambda e: e.matmul(RS(out), RS(lhsT), RS(rhs), start=start, stop=stop), reads, writes,
                   cost=8.0 + 0.43 * max(nfree(out), 100))

        def tr(out, in_, ident, reads, writes):
            S.emit("pe", lambda e: e.transpose(RS(out), in_, ident), reads, writes, cost=70.0)

        def act(out, in_, func, reads, writes, scale=1.0, bias=0.0, accum=None):
            c = 170.0 + 0.82 * nfree(out)
            if accum is None:
                S.emit("act", lambda e: e.activation(RS(out), RS(in_), func, bias=bias, scale=scale), reads, writes, cost=c)
            else:
                c = 170.0 + 0.82 * nfree(in_)
                S.emit("act", lambda e: e.activation(RS(out), RS(in_), func, bias=bias, scale=scale, accum_out=accum),
                       reads, writes, cost=c)

        def tt(eng, out, in0, in1, op, reads, writes):
            S.emit(eng, lambda e: e.tensor_tensor(RS(out), RS(in0), RS(in1), op), reads, writes, cost=vcost(eng, out, in0, in1))

        def ts(eng, out, in0, s1, s2, op0, op1, reads, writes):
            if s2 is None:
                S.emit(eng, lambda e: e.tensor_scalar(RS(out), RS(in0), s1, None, op0), reads, writes, cost=vcost(eng, out))
            else:
                S.emit(eng, lambda e: e.tensor_scalar(RS(out), RS(in0), s1, s2, op0, op1), reads, writes,
                       cost=vcost(eng, out))

        def stt(eng, out, in0, scalar, in1, op0, op1, reads, writes):
            S.emit(eng, lambda e: e.scalar_tensor_tensor(RS(out), RS(in0), scalar, RS(in1), op0, op1), reads, writes,
                   cost=vcost(eng, out))

        def recip(out, in_, reads, writes):
            S.emit("dve", lambda e: e.reciprocal(out, in_), reads, writes, cost=100.0 + 5.5 * nfree(out))

        def cp(eng, out, in_, reads, writes):
            S.emit(eng, lambda e: e.tensor_copy(RS(out), RS(in_)), reads, writes, cost=vcost(eng, out))

        pool_dmas = []

        def dma(q, out, in_, reads, writes, key):
            op = S.emit(q, lambda e: e.dma_start(out=out, in_=in_), reads, writes, dma=key,
                        cost=2500.0 + 0.02 * 128 * nfree(out) if out.shape[0] == 128 else 4000.0)
            if q == "pool":
                if len(pool_dmas) >= MAX_SWDGE_INFLIGHT:
                    op.deps.append(pool_dmas[-MAX_SWDGE_INFLIGHT])
                pool_dmas.append(op)

        A = S.A
        R = S.R

        dma("pool", V(cm, 0, [(128, 3), (1, 128)]), cmat.rearrange("k p c -> p k c"), [], [A("cm")], "c0")
        dma("pool", V(bm, 0, [(1, 512)]), bmask, [], [A("bm")], "c1")
        dma("sp", V(ng, 0, [(1, 8)]), ngd, [], [A("ng")], "c2")
        dma("sp", V(gc, 0, [(1, 4)]), gcd, [], [A("gc")], "c3")
        ts("dve", V(gc, 0, [(2, 2)]), V(gc, 0, [(2, 2)]), 0.125, None, ALU.mult, None, [A("gc")], [A("gc")])
        IDENT = V(cm, 0, [(1, 128)])
        PERM = V(cm, 128, [(1, 128)])
        BD = V(cm, 256, [(1, 128)])

        wstate = {"n": 0}

        def load_w(src_ap_fn, nchunk=8, ncol=128):
            s = wstate["n"] % NSLOT
            wstate["n"] += 1
            dma("pool", V(Wr, s * 1024, [(ncol, nchunk), (1, ncol)]), src_ap_fn, [], [A("W", s)], "w%d" % s)
            return s

        w_in_v = w_in.rearrange("(c p) e -> p c e", p=128)

        def w_in_cols(col0, ncol=128):
            return w_in_v[:, :, col0:col0 + ncol]

        def hT_tiles(t0, t1_):
            return [A("hT", i) for i in range(t0 // 128, (t1_ - 1) // 128 + 1)]

        acc_rr = {"n": 0}
        aux_rr = {"n": 0}
        sc_rr = {"n": 0}
        st_rr = {"n": 0}
        pt_rr = {"n": 0}

        banks = [psA[0], psA[1], psX[0], psX[1], psS[0], psS[1], psO0, psT]

        def next_bank():
            vb = VB()
            S.vbs_all.append(vb)
            return vb, vb.atom

        next_acc = next_bank
        next_aux = next_bank
        next_st = next_bank

        def proj_fm(slot, t0, N):
            ps, pa = next_acc()
            for c in range(8):
                mm(V(ps, 0, [(1, N)]), V(Wr, slot * 1024 + c * 128, [(1, 128)]),
                   V(hT, c * W + t0, [(1, N)]), c == 0, c == 7,
                   [A("W", slot)] + hT_tiles(t0, t0 + N), [pa])
            return ps, pa

        def rsqrt_chain(ps, pa, N, sl):
            o = sl * 512
            act(V(sq, o, [(1, N)]), V(ps, 0, [(1, N)]), AF.Square, [pa], [A("sq", sl)])
            px, pxa = next_aux()
            mm(V(px, 0, [(1, N)]), BD, V(sq, o, [(1, N)]), True, True, [A("cm"), A("sq", sl)], [pxa])
            act(V(rs, o, [(1, N)]), V(px, 0, [(1, N)]), AF.Ln, [pxa], [A("rs", sl)], scale=1.0, bias=EPS)
            act(V(rs, o, [(1, N)]), V(rs, o, [(1, N)]), AF.Exp, [A("rs", sl)], [A("rs", sl)], scale=-0.5)

        def qk_tile(slot, t0, N, gcol, dst_off, d, sub, n_off, cidx):
            ps, pa = proj_fm(slot, t0, N)
            sl = sc_rr["n"] % NSC
            sc_rr["n"] += 1
            o = sl * 512
            rsqrt_chain(ps, pa, N, sl)
            if d == 0:
                c0 = dst_off + (t0 - n_off)
                stt("dve", V(AR, c0, [(1, N)]), V(ps, 0, [(1, N)]), V(gc, gcol, [(1, 1)]), V(rs, o, [(1, N)]),
                    ALU.mult, ALU.mult, [pa, A("gc"), A("rs", sl)], R("AR", c0, c0 + N))
                return
            stt("dve", V(qn, o, [(1, N)]), V(ps, 0, [(1, N)]), V(gc, gcol, [(1, 1)]), V(rs, o, [(1, N)]),
                ALU.mult, ALU.mult, [pa, A("gc"), A("rs", sl)], [A("qn", sl)])
            px, pxa = next_aux()
            mm(V(px, 0, [(1, N)]), PERM, V(qn, o, [(1, N)]), True, True, [A("cm"), A("qn", sl)], [pxa])
            tt("pool", V(t1, o, [(1, N)]), V(qn, o, [(1, N)]), V(RT, t0, [(1, N)]), ALU.mult,
               [A("qn", sl)] + R("RT", t0, t0 + N), [A("t1", sl)])
            tt("dve", V(t2, o, [(1, N)]), V(px, 0, [(1, N)]), V(RT, W + t0, [(1, N)]), ALU.mult,
               [pxa] + R("RT", W + t0, W + t0 + N), [A("t2", sl)])
            n0 = t0 // d - n_off
            cnt = N // d
            wr = []
            for r in range(d):
                wr += R("AR", dst_off + r * sub + n0, dst_off + r * sub + n0 + cnt)
            tt("pool" if (ROPE_ADD_POOL and (sc_rr["n"] % ROPE_ADD_POOL == 0)) else "dve", V(AR, dst_off + n0, [(sub, d), (1, cnt)]),
               V(t1, o, [(1, d), (d, cnt)]), V(t2, o, [(1, d), (d, cnt)]), ALU.add,
               [A("t1", sl), A("t2", sl)], wr)

        QT, KT, VV, SG = 0, 2048, 6144, 14336

        def v_blocks(slot, nblk, tok_fn, d, vf_col):
            cp("dve", V(AR, VV + 64, [(256, nblk), (64, 2), (1, 64)]),
               V(vfl, vf_col, [(1, nblk), (0, 2), (0, 64)]), [A("vfl")], [A("VFLG")])
            for b0 in range(0, nblk, 4):
                nb = min(4, nblk - b0)
                ps, pa = next_acc()
                for jj in range(nb):
                    tk = tok_fn(b0 + jj)
                    for c in range(8):
                        mm(V(ps, jj * 128, [(1, 128)]), V(hT, c * W + tk, [(d, 128)]),
                           V(Wr, slot * 1024 + c * 128, [(1, 128)]), c == 0, c == 7,
                           [A("W", slot)] + hT_tiles(tk, tk + 127 * d + 1), [pa])
                c0 = VV + b0 * 256
                wr = R("AR", c0, c0 + nb * 256)
                cp("dve", V(AR, c0, [(256, nb), (192, 2), (1, 64)]), V(ps, 0, [(128, nb), (64, 2), (1, 64)]),
                   [pa], wr)

        def gate_tiles(slot):
            for i in range(4):
                t0 = Q0 + 512 * i
                ps, pa = proj_fm(slot, t0, 512)
                sl = sc_rr["n"] % NSC
                sc_rr["n"] += 1
                o = sl * 512
                act(V(t1, o, [(1, 512)]), V(ps, 0, [(1, 512)]), AF.Exp, [pa], [A("t1", sl)], scale=-1.0)
                act(V(t1, o, [(1, 512)]), V(t1, o, [(1, 512)]), AF.Ln, [A("t1", sl)], [A("t1", sl)], bias=1.0)
                act(V(t1, o, [(1, 512)]), V(t1, o, [(1, 512)]), AF.Exp, [A("t1", sl)], [A("t1", sl)], scale=-1.0)
                c0 = SG + 512 * i
                tt("dve", V(AR, c0, [(1, 512)]), V(ps, 0, [(1, 512)]), V(t1, o, [(1, 512)]), ALU.mult,
                   [pa, A("t1", sl)], R("AR", c0, c0 + 512))

        mul_rr = {"n": 0}
        nat_rr = {"n": 0}

        def mul_eng():
            return "dve"

        OUTER = list(range(0, 8)) + list(range(24, 32))
        INNER = list(range(8, 24))

        def phase_a(ch, tiles):
            for ti in tiles:
                xs = ti % 4
                dma("sp", V(XA, xs * 1024, [(1, 1024)]), xw[ch, ti * 128:(ti + 1) * 128, :], [], [A("XA", xs)],
                    "x%d" % xs)
                ss = ti % 2
                act(V(xn, ss * 1024, [(1, 1024)]), V(XA, xs * 1024, [(1, 1024)]), AF.Square, [A("XA", xs)],
                    [A("xn", ss), A("st", ss)], accum=V(st_, ss, [(1, 1)]))
                act(V(st_, 2 + ss, [(1, 1)]), V(st_, ss, [(1, 1)]), AF.Ln, [A("st", ss)], [A("st2", ss)],
                    scale=1.0 / D, bias=EPS)
                act(V(st_, 4 + ss, [(1, 1)]), V(st_, 2 + ss, [(1, 1)]), AF.Exp, [A("st2", ss)], [A("st3", ss)],
                    scale=-0.5)
                ts("dve", V(xn, ss * 1024, [(1, 1024)]), V(XA, xs * 1024, [(1, 1024)]), V(st_, 4 + ss, [(1, 1)]),
                   None, ALU.mult, None, [A("XA", xs), A("st3", ss)], [A("xn", ss)])
                vbT, vbTa = next_bank()
                for c in range(8):
                    tr(VT(vbT, c * 128, (c + 1) * 128), V(xn, ss * 1024 + c * 128, [(1, 128)]), IDENT,
                       [A("xn", ss), A("cm")], [vbTa])
                tt("dve", V(hT, ti * 128, [(W, 8), (1, 128)]), VT8(vbT),
                   V(ng, 0, [(1, 8), (0, 128)]), ALU.mult, [vbTa, A("ng")], [A("hT", ti)])

        for ch in range(NCH):
            if ch == 0:
                dma("sp", V(vfl, 0, [(1, NVF)]), vfld[ch], [], [A("vfl")], "vf")
                phase_a(ch, OUTER)
            phase_a(ch, INNER)

            dma("pool", V(RT, 0, [(1, W)]), ropec[ch], [], R("RT", 0, W), "rc")
            dma("pool", V(RT, W, [(1, W)]), ropes[ch], [], R("RT", W, 2 * W), "rs")
            for jp in range(4):
                cb = jp * 128
                if ch == 0:
                    tv = (1, 0, 0)
                elif ch == 1:
                    tv = (0, 0, 1)
                else:
                    tv = (2, 0, 3)
                s_q = load_w(w_in_cols(QA + cb))
                s_k = load_w(w_in_cols(KA + cb))
                s_v = load_w(w_in_cols(VA + cb))
                s_g = load_w(w_in_cols(GA + cb))
                for i in range(4):
                    qk_tile(s_q, Q0 + 512 * i, 512, 0, QT, 0, 0, Q0, 0)
                for i in range(5):
                    qk_tile(s_k, 768 + 512 * i, 512, 1, KT, 0, 0, 0, 0)
                v_blocks(s_v, 20, lambda b: 768 + 128 * b, 1, VF_OFF["na"])
                gate_tiles(s_g)
                def load_tab(var, h2, dst):
                    xs = nat_rr["n"] % 4
                    nat_rr["n"] += 1
                    dma("sp", V(XA, xs * 1024, [(1, 960)]), natab[jp, var, :, h2 * 960:(h2 + 1) * 960],
                        [], [A("XA", xs)], "x%d" % xs)
                    act(V(NAT, dst * 960, [(1, 960)]), V(XA, xs * 1024, [(1, 960)]), AF.Exp, [A("XA", xs)],
                        [A("NAT", dst)])
                for h2 in range(2):
                    load_tab(tv[1], h2, h2)
                    load_tab(tv[0], h2, 2 + h2)
                for hh in range(2):
                    pq = hh * 64
                    pd = (1 - hh) * 64
                    for rg in range(8):
                        if rg == 7:
                            load_tab(tv[2], hh, 2 + hh)
                        tslot = (2 + hh) if rg in (0, 7) else hh
                        psO, psOa = next_bank()
                        for pp in range(3):
                            stp, sta = next_st()
                            for slot_i, kb in ((0, 2 * pp + 1), (1, 2 * pp)):
                                j = 2 * rg + kb
                                kc = KT + 768 + 128 * j
                                mm(V(stp, slot_i * 256, [(1, 256)]), V(AR, kc, [(1, 128)], pq, 64),
                                   V(AR, QT + 256 * rg, [(1, 256)], pq, 64), True, True,
                                   R("AR", kc, kc + 128) + R("AR", QT + 256 * rg, QT + 256 * rg + 256), [sta])
                            pts = pt_rr["n"] % 4
                            pt_rr["n"] += 1
                            act(V(PT, pts * 512, [(1, 512)]), V(stp, 0, [(1, 512)]), AF.Exp, [sta], [A("PT", pts)])
                            tc0 = tslot * 960 + (9 - 4 * pp) * 64
                            tt(mul_eng(), V(PT, pts * 512, [(256, 2), (1, 256)]),
                               V(PT, pts * 512, [(256, 2), (1, 256)]), V(NAT, tc0, [(128, 2), (1, 256)]), ALU.mult,
                               [A("PT", pts), A("NAT", tslot)], [A("PT", pts)])
                            for slot_i, kb in ((0, 2 * pp + 1), (1, 2 * pp)):
                                j = 2 * rg + kb
                                vc = VV + j * 256 + hh * 128
                                first = (pp == 0 and slot_i == 0)
                                last = (pp == 2 and slot_i == 1)
                                mm(V(psO, 0, [(1, 256)]), V(AR, vc, [(1, 128)]),
                                   V(PT, pts * 512 + slot_i * 256, [(1, 256)]), first, last,
                                   R("AR", vc, vc + 128) + [A("PT", pts), A("VFLG")], [psOa])
                        sl = sc_rr["n"] % NSC
                        sc_rr["n"] += 1
                        o = sl * 512
                        act(V(rs, o, [(1, 256)], pq, 64), V(psO, 0, [(1, 256)], pd, 64), AF.Ln, [psOa],
                            [A("rs", sl)])
                        act(V(rs, o, [(1, 256)], pq, 64), V(rs, o, [(1, 256)], pq, 64), AF.Exp, [A("rs", sl)],
                            [A("rs", sl)], scale=-1.0)
                        tt("dve", V(t1, o, [(1, 256)], pq, 64), V(psO, 0, [(1, 256)], pq, 64),
                           V(rs, o, [(1, 256)], pq, 64), ALU.mult, [psOa, A("rs", sl)], [A("t1", sl)])
                        oc = (0 * 4 + jp) * NQ + 256 * rg
                        tt(FIN_ENG, V(og, oc, [(1, 256)], pq, 64), V(t1, o, [(1, 256)], pq, 64),
                           V(AR, SG + 256 * rg, [(1, 256)], pq, 64), ALU.mult,
                           [A("t1", sl)] + R("AR", SG + 256 * rg, SG + 256 * rg + 256),
                           [A("og", 0, jp, rg // 2)])

                s_g = load_w(w_in_cols(GB + cb))
                gate_tiles(s_g)
                for gi, (d, qc, kc_, vc_) in enumerate(DIL):
                    sub = W // d
                    qsub = NQ // d
                    nj = 16 // d + 1
                    s_q = load_w(w_in_cols(qc + cb))
                    s_k = load_w(w_in_cols(kc_ + cb))
                    s_v = load_w(w_in_cols(vc_ + cb))
                    for i in range(4):
                        qk_tile(s_q, Q0 + 512 * i, 512, 2, QT, d, qsub, Q0 // d, 0)
                    if d == 1:
                        ktiles = [(960 + 512 * i, 512) for i in range(4)] + [(3008, 128)]
                    elif d == 4:
                        ktiles = [(768 + 512 * i, 512) for i in range(5)]
                    else:
                        ktiles = [(512 * i, 512) for i in range(8)]
                    for (t0, N) in ktiles:
                        qk_tile(s_k, t0, N, 3, KT, d, sub, 0, 0)
                    kbase = Q0 // d - 64

                    def tokfn(b, d=d, nj=nj, kbase=kbase):
                        r, j = divmod(b, nj)
                        return (kbase + 128 * j) * d + r
                    v_blocks(s_v, d * nj, tokfn, d, VF_OFF[d])
                    nb_r = 16 // d
                    for hh in range(2):
                        pq = hh * 64
                        for qp in range(8):
                            stp, sta = next_st()
                            psO, psOa = next_bank()
                            info = []
                            for qi in range(2):
                                qb = 2 * qp + qi
                                r, b = divmod(qb, nb_r)
                                info.append((r, b))
                                qcol = QT + r * qsub + 128 * b
                                for half in range(2):
                                    j = b + half
                                    kcol = KT + r * sub + kbase + 128 * j
                                    mm(V(stp, qi * 256 + half * 128, [(1, 128)]), V(AR, kcol, [(1, 128)], pq, 64),
                                       V(AR, qcol, [(1, 128)], pq, 64), True, True,
                                       R("AR", kcol, kcol + 128) + R("AR", qcol, qcol + 128), [sta])
                            pts = pt_rr["n"] % 4
                            pt_rr["n"] += 1
                            act(V(PT, pts * 512, [(1, 512)]), V(stp, 0, [(1, 512)]), AF.Exp, [sta], [A("PT", pts)])
                            tt(mul_eng(), V(PT, pts * 512, [(1, 512)]), V(PT, pts * 512, [(1, 512)]),
                               V(bm, 0, [(1, 512)]), ALU.mult, [A("PT", pts), A("bm")], [A("PT", pts)])
                            for qi in range(2):
                                r, b = info[qi]
                                for half in range(2):
                                    kbi = r * nj + b + half
                                    vc = VV + kbi * 256 + hh * 128
                                    mm(V(psO, qi * 128, [(1, 128)]), V(AR, vc, [(1, 128)]),
                                       V(PT, pts * 512 + qi * 256 + half * 128, [(1, 128)]), half == 0, half == 1,
                                       R("AR", vc, vc + 128) + [A("PT", pts), A("VFLG")], [psOa])
                            (r0, b0), (r1, b1) = info
                            off0 = r0 + 128 * b0 * d
                            off1 = r1 + 128 * b1 * d
                            accv = V(XA, hh * 2048 + off0, [(off1 - off0, 2), (d, 128)])
                            acca = [A("XA", 2 * hh), A("XA", 2 * hh + 1)]
                            if gi == 0:
                                cp("dve", accv, V(psO, 0, [(128, 2), (1, 128)]), [psOa], acca)
                            else:
                                tt("dve", accv, V(psO, 0, [(128, 2), (1, 128)]), accv, ALU.add,
                                   [psOa] + acca, acca)
                for hh in range(2):
                    pq = hh * 64
                    pd = (1 - hh) * 64
                    for i in range(4):
                        sl = sc_rr["n"] % NSC
                        sc_rr["n"] += 1
                        o = sl * 512
                        acca = [A("XA", 2 * hh), A("XA", 2 * hh + 1)]
                        act(V(rs, o, [(1, 512)], pq, 64), V(XA, hh * 2048 + 512 * i, [(1, 512)], pd, 64), AF.Ln,
                            acca, [A("rs", sl)])
                        act(V(rs, o, [(1, 512)], pq, 64), V(rs, o, [(1, 512)], pq, 64), AF.Exp, [A("rs", sl)],
                            [A("rs", sl)], scale=-1.0)
                        tt("dve", V(t1, o, [(1, 512)], pq, 64), V(XA, hh * 2048 + 512 * i, [(1, 512)], pq, 64),
                           V(rs, o, [(1, 512)], pq, 64), ALU.mult, acca + [A("rs", sl)], [A("t1", sl)])
                        oc = (1 * 4 + jp) * NQ + 512 * i
                        tt(FIN_ENG, V(og, oc, [(1, 512)], pq, 64), V(t1, o, [(1, 512)], pq, 64),
                           V(AR, SG + 512 * i, [(1, 512)], pq, 64), ALU.mult,
                           [A("t1", sl)] + R("AR", SG + 512 * i, SG + 512 * i + 512), [A("og", 1, jp, i)])

            if ch + 1 < NCH:
                dma("sp", V(vfl, 0, [(1, NVF)]), vfld[ch + 1], [], [A("vfl")], "vf")
                phase_a(ch + 1, OUTER)
            dma("pool", V(RT, 0, [(1024, 8), (1, 1024)]), wout.rearrange("(c p) e -> p c e", p=128), [],
                R("RT", 0, 8192), "wo")
            for ec in range(8):
                s_ma = load_w(w_in_cols(MA + ec * 128))
                s_mb = load_w(w_in_cols(MB + ec * 128))
                s_ba = load_w(wba.rearrange("(c p) e -> p c e", p=128)[:, :, ec * 128:(ec + 1) * 128], nchunk=4)
                s_bb = load_w(wbb.rearrange("(c p) e -> p c e", p=128)[:, :, ec * 128:(ec + 1) * 128], nchunk=4)
                for tq in range(4):
                    t0 = Q0 + 512 * tq
                    pma, pmaa = proj_fm(s_ma, t0, 512)
                    pmb, pmba = proj_fm(s_mb, t0, 512)
                    pba, pbaa = next_aux()
                    pbb, pbba = next_aux()
                    for ab, (pb_, pba_, sslot) in enumerate(((pba, pbaa, s_ba), (pbb, pbba, s_bb))):
                        for jp in range(4):
                            mm(V(pb_, 0, [(1, 512)]), V(Wr, sslot * 1024 + jp * 128, [(1, 128)]),
                               V(og, (ab * 4 + jp) * NQ + 512 * tq, [(1, 512)]), jp == 0, jp == 3,
                               [A("W", sslot), A("og", ab, jp, tq)], [pba_])
                    sl = sc_rr["n"] % NSC
                    sc_rr["n"] += 1
                    o = sl * 512
                    for (pm, pma_, pb_, pba_, tb, tbn) in ((pma, pmaa, pba, pbaa, t1, "t1"),
                                                          (pmb, pmba, pbb, pbba, t2, "t2")):
                        act(V(tb, o, [(1, 512)]), V(pm, 0, [(1, 512)]), AF.Exp, [pma_], [A(tbn, sl)], scale=-1.0)
                        act(V(tb, o, [(1, 512)]), V(tb, o, [(1, 512)]), AF.Ln, [A(tbn, sl)], [A(tbn, sl)], bias=1.0)
                        act(V(tb, o, [(1, 512)]), V(tb, o, [(1, 512)]), AF.Exp, [A(tbn, sl)], [A(tbn, sl)],
                            scale=-1.0)
                        tt("dve", V(tb, o, [(1, 512)]), V(pb_, 0, [(1, 512)]), V(tb, o, [(1, 512)]), ALU.mult,
                           [pba_, A(tbn, sl)], [A(tbn, sl)])
                    mc = ec * NQ + 512 * tq
                    tt("pool" if MERGE_ADD_POOL else "dve", V(AR, mc, [(1, 512)]), V(t1, o, [(1, 512)]), V(t2, o, [(1, 512)]), ALU.add,
                       [A("t1", sl), A("t2", sl)], R("AR", mc, mc + 512) + [A("VFLG")])
            for ti in range(16):
                xs = ti % 4
                tok = Q0 + 128 * ti
                dma("sp", V(XA, xs * 1024, [(1, 1024)]), xw[ch, tok:tok + 128, :], [], [A("XA", xs)], "x%d" % xs)
                for half in range(2):
                    stp, sta = next_st()
                    for ec in range(8):
                        mc = ec * NQ + 128 * ti
                        mm(V(stp, 0, [(1, 512)]), V(AR, mc, [(1, 128)]),
                           V(RT, ec * 1024 + half * 512, [(1, 512)]), ec == 0, ec == 7,
                           R("AR", mc, mc + 128) + R("RT", ec * 1024 + half * 512, ec * 1024 + half * 512 + 512)
                           + [A("VFLG")], [sta])
                    tt("dve", V(XA, xs * 1024 + half * 512, [(1, 512)]), V(stp, 0, [(1, 512)]),
                       V(XA, xs * 1024 + half * 512, [(1, 512)]), ALU.add, [sta, A("XA", xs)], [A("XA", xs)])
                dma("sp", yo[ch, 128 * ti:128 * (ti + 1), :], V(XA, xs * 1024, [(1, 1024)]), [A("XA", xs)], [],
                    "x%d" % xs)

        if DO_SCHED:
            S.schedule(window=WINDOW, nphys=NPHYS)
        else:
            S.est_ns = 0
        print('[sched] est_us=%.1f ops=%s' % (S.est_ns / 1e3, {e: len(S.ops[e]) for e in S.ENGS}), flush=True)
        pos = {}
        for e in S.ENGS:
            for i_, op in enumerate(S.ops[e]):
                pos[id(op)] = i_
        waits_of = {}
        for e in S.ENGS:
            seen = {}
            for op in S.ops[e]:
                grp = {}
                for p in op.deps:
                    kk = p.eng if p.dma is None else ("d", p.dma)
                    q = grp.get(kk)
                    if q is None or pos[id(p)] > pos[id(q)]:
                        grp[kk] = p
                wl = []
                for kk, p in grp.items():
                    if seen.get(kk, -1) >= pos[id(p)]:
                        continue
                    seen[kk] = pos[id(p)]
                    p.needed = True
                    wl.append(p)
                waits_of[id(op)] = wl
        sems = {}
        for e in S.ENGS:
            sems[e] = es.enter_context(nc.semaphore("sem_" + e))
        dsems = {}
        for k in S.dma_keys:
            dsems[k] = es.enter_context(nc.semaphore("dsem_" + k))
        dcount = {k: 0 for k in S.dma_keys}
        nsig = {}
        for e in S.ENGS:
            cnt = 0
            for op in S.ops[e]:
                if op.dma is not None:
                    continue
                if op.needed:
                    cnt += 1
                    op.sig = (sems[e], cnt)
            nsig[e] = cnt
        for e in S.ENGS:
            for op in S.ops[e]:
                if op.dma is not None:
                    dcount[op.dma] += 16
                    op.sig = (dsems[op.dma], dcount[op.dma])
        print('[sync] signalling ops per engine:', nsig, flush=True)

        def run_engine(ename, eh):
            for op in S.ops[ename]:
                for p in waits_of[id(op)]:
                    eh.wait_ge(p.sig[0], p.sig[1])
                ins = op.fn(eh)
                if op.dma is not None:
                    ins.then_inc(op.sig[0], 16)
                elif op.needed:
                    ins.then_inc(op.sig[0], 1)
            if ename == "sp":
                for k, v in dcount.items():
                    if v > 0:
                        eh.wait_ge(dsems[k], v)

        with nc.Block() as block:
            @block.tensor
            def _(e):
                run_engine("pe", e)

            @block.scalar
            def _(e):
                run_engine("act", e)

            @block.vector
            def _(e):
                run_engine("dve", e)

            @block.gpsimd
            def _(e):
                run_engine("pool", e)

            @block.sync
            def _(e):
                run_engine("sp", e)
    return nc


def _na_tables(rel_bias):
    rb = np.asarray(rel_bias, np.float32)
    p = np.arange(128)
    half = p // 64
    ck = p % 64
    s = np.arange(15)
    cq = np.arange(64)
    drow = 7 - s[None, :] + half[:, None]
    cstart = np.clip(cq - 8, 0, 48)
    colok = (ck[:, None] >= cstart[None, :]) & (ck[:, None] < cstart[None, :] + 16)
    dcol = np.clip(ck[:, None] - cq[None, :], -15, 15) + 15
    out = np.full((4, 4, 128, 2, 15, 64), -30000.0, np.float32)
    for var in range(2):
        if var == 0:
            rowok = (drow >= -4) & (drow <= 3)
        else:
            rowok = (drow >= -7) & (drow <= 7)
        ok = rowok[:, :, None] & colok[:, None, :]
        ridx = np.clip(drow + 7, 0, 14)
        for h in range(8):
            vals = rb[h][ridx[:, :, None], dcol[:, None, :]]
            out[h // 2, var, :, h % 2] = np.where(ok, vals, np.float32(-30000.0))
    return out.reshape(4, 4, 128, 1920)


def _rope_tables(pos):
    inv = (10000.0 ** (-np.arange(0, 64, 2, dtype=np.float32) / np.float32(64))).astype(np.float32)
    ang = (pos.astype(np.float32)[None, :] * inv[:, None]).astype(np.float32)
    c = np.cos(ang.astype(np.float64)).astype(np.float32)
    s = np.sin(ang.astype(np.float64)).astype(np.float32)
    C = np.tile(c, (4, 1))
    Sg = np.concatenate([-s, s, -s, s], axis=0)
    return C, Sg


def _valid_flags(valid):
    out = np.zeros((128, NVF), np.float32)
    i = np.arange(128)
    for j in range(20):
        out[:, VF_OFF["na"] + j] = valid[768 + 128 * j + i]
    for d in (1, 4, 16):
        nj = 16 // d + 1
        kbase = Q0 // d - 64
        for r in range(d):
            for j in range(nj):
                out[:, VF_OFF[d] + r * nj + j] = valid[(kbase + 128 * j + i) * d + r]
    return out


_CACHE = {}


def kernel(x_prompt, x_sample, norm_gain, w_in, qn_a, kn_a, rel_bias_a, qn_b, kn_b, w_branch_a, w_branch_b, w_out):
    x_prompt = np.asarray(x_prompt, np.float32)
    x_sample = np.asarray(x_sample, np.float32)
    if "nc" not in _CACHE:
        _CACHE["nc"] = build_program()
    nc = _CACHE["nc"]

    w_in2 = np.ascontiguousarray(np.asarray(w_in, np.float32)[0])
    wba = np.ascontiguousarray(np.asarray(w_branch_a, np.float32)[0])
    wbb = np.ascontiguousarray(np.asarray(w_branch_b, np.float32)[0])
    wo = np.ascontiguousarray(np.asarray(w_out, np.float32)[0])
    ng = np.ascontiguousarray(np.asarray(norm_gain, np.float32)[0].reshape(8, 128).T)
    gcols = np.stack([np.tile(np.asarray(a, np.float32)[0], 2) for a in (qn_a, qn_b, kn_a, kn_b)], axis=1)
    gcols = np.ascontiguousarray(gcols[:, [0, 2, 1, 3]])
    nat = _na_tables(np.asarray(rel_bias_a, np.float32)[0])
    ident = np.eye(128, dtype=np.float32)
    perm = np.zeros((128, 128), np.float32)
    for m in range(128):
        perm[m, m + 32 if (m % 64) < 32 else m - 32] = 1.0
    bd = np.zeros((128, 128), np.float32)
    bd[:64, :64] = 1.0 / 64
    bd[64:, 64:] = 1.0 / 64
    cmat = np.stack([ident, perm, bd])
    kk = np.arange(128)[:, None]
    qq = np.arange(128)[None, :]
    m1 = np.concatenate([(qq <= kk), (qq >= kk)], axis=1).astype(np.float32)
    bmask = np.ascontiguousarray(np.concatenate([m1, m1], axis=1))

    in_maps = []
    for c in range(NCORES):
        xw = np.zeros((NCH, W, D), np.float32)
        vfl = np.zeros((NCH, 128, NVF), np.float32)
        rc = np.zeros((NCH, 128, W), np.float32)
        rsn = np.zeros((NCH, 128, W), np.float32)
        for ch, (src, L, q0) in enumerate(((x_prompt[c], 4096, 0), (x_prompt[c], 4096, 2048),
                                           (x_sample[0], 16384, 2048 * c))):
            lo = q0 - Q0
            pos = np.arange(lo, lo + W)
            valid = ((pos >= 0) & (pos < L))
            a = max(lo, 0)
            b = min(lo + W, L)
            xw[ch, a - lo:b - lo] = src[a:b]
            vfl[ch] = _valid_flags(valid.astype(np.float32))
            C, Sg = _rope_tables(np.clip(pos, 0, L - 1))
            rc[ch] = C
            rsn[ch] = Sg
        natc = nat.copy()
        natc[:, 2] = nat[:, 1] if c == 0 else nat[:, 0]
        natc[:, 3] = nat[:, 1] if c == NCORES - 1 else nat[:, 0]
        in_maps.append({"xw": xw, "w_in": w_in2, "wba": wba, "wbb": wbb, "wout": wo, "ng": ng, "gcols": gcols,
                        "ropec": rc, "ropes": rsn, "natab": natc, "vfl": vfl, "cmat": cmat, "bmask": bmask})

    res = run_bass_kernel_spmd(nc, in_maps, core_ids=list(range(NCORES)))
    y_prompt = np.empty((8, 4096, D), np.float32)
    y_sample = np.empty((1, 16384, D), np.float32)
    for c in range(NCORES):
        yo = res.results[c]["yo"]
        y_prompt[c, 0:2048] = yo[0]
        y_prompt[c, 2048:4096] = yo[1]
        y_sample[0, 2048 * c:2048 * (c + 1)] = yo[2]
    return (y_prompt, y_sample)
```

```python
import numpy as np
import concourse.bass as bass
import concourse.mybir as mybir
from concourse.bass_utils import run_bass_kernel_spmd

F32 = mybir.dt.float32
BF16 = mybir.dt.bfloat16
AF = mybir.ActivationFunctionType
ALU = mybir.AluOpType

NCORES = 8
NPHYS = 8
MAX_SWDGE_INFLIGHT = 3
LAT = 300.0
FIN_ENG = "dve"
ROPE_ADD_POOL = 0
MERGE_ADD_POOL = 0
NSC = 2
WINDOW = 256
DO_SCHED = True
D = 1024
DIN = 9216
W = 4096
Q0 = 1024
NQ = 2048
NCH = 3
EPS = 1e-6
QA, KA, VA, GA = 0, 512, 1024, 1536
DIL = ((1, 2048, 2560, 3072), (4, 3584, 4096, 4608), (16, 5120, 5632, 6144))
GB, MA, MB = 6656, 7168, 8192
NVF = 20 + 17 + 20 + 32
VF_OFF = {"na": 0, 1: 20, 4: 37, 16: 57}


class Atom:
    __slots__ = ("lw", "rd", "vb")

    def __init__(self):
        self.lw = None
        self.rd = {}
        self.vb = None


class VB:
    __slots__ = ("atom", "ops", "phys", "nsched", "lastfin")

    def __init__(self):
        self.atom = Atom()
        self.atom.vb = self
        self.ops = []
        self.phys = None
        self.nsched = 0
        self.lastfin = 0.0


class LazyAP:
    __slots__ = ("vb", "off", "dims", "p0", "pn", "shape", "bf")

    def __init__(self, vb, off, dims, p0, pn, bf=None):
        self.vb = vb
        self.off = off
        self.dims = dims
        self.p0 = p0
        self.pn = pn
        self.bf = bf
        self.shape = (pn,) + tuple(c for _, c in dims)


class Op:
    __slots__ = ("eng", "fn", "deps", "odeps", "dma", "needed", "sig", "idx", "cost", "fin", "gi", "vbs")

    def __init__(self, eng, fn, dma):
        self.eng = eng
        self.fn = fn
        self.dma = dma
        self.deps = []
        self.odeps = []
        self.needed = False
        self.sig = None
        self.cost = 300.0
        self.fin = None
        self.vbs = []


class Sched:
    ENGS = ("pe", "act", "dve", "pool", "sp")

    def __init__(self):
        self.ops = {e: [] for e in self.ENGS}
        self.atoms = {}
        self.dma_keys = {}
        self.gcount = 0
        self.vbs_all = []

    def A(self, *key):
        a = self.atoms.get(key)
        if a is None:
            a = self.atoms[key] = Atom()
        return a

    def R(self, name, a, b, g=128):
        return [self.A(name, i) for i in range(a // g, (b - 1) // g + 1)]

    def emit(self, eng, fn, reads=(), writes=(), dma=None, cost=300.0):
        op = Op(eng, fn, dma)
        op.idx = len(self.ops[eng])
        op.cost = cost
        op.gi = self.gcount
        self.gcount += 1
        deps = {}
        odeps = {}

        def add(p):
            if p is None:
                return
            if p.eng == "pe" and eng == "pe" and p.dma is None:
                odeps[id(p)] = p
                return
            deps[id(p)] = p

        for b in reads:
            add(b.lw)
        for b in writes:
            add(b.lw)
            for p in b.rd.values():
                add(p)
        op.deps = list(deps.values())
        op.odeps = list(odeps.values())
        for b in list(reads) + list(writes):
            if b.vb is not None and b.vb not in op.vbs:
                op.vbs.append(b.vb)
                b.vb.ops.append(op)
        rk = ("e", id(op))
        for b in reads:
            b.rd[rk] = op
        for b in writes:
            b.lw = op
            b.rd = {}
        self.ops[eng].append(op)
        if dma is not None:
            self.dma_keys.setdefault(dma, 0)
        return op

    def schedule(self, window=48, lat=LAT, nphys=7):
        rem = {e: list(self.ops[e]) for e in self.ENGS}
        out = {e: [] for e in self.ENGS}
        free = {e: 0.0 for e in self.ENGS}
        total = sum(len(v) for v in rem.values())
        done = 0
        rdy = {}
        blk = {}
        issue = {"sp": 100.0, "pool": 1200.0}
        occ = [None] * nphys

        vbs_all = self.vbs_all
        vptr = [0]
        RESERVE = 2

        def oldest_vb():
            i = vptr[0]
            while i < len(vbs_all) and vbs_all[i].phys is not None:
                i += 1
            vptr[0] = i
            return vbs_all[i] if i < len(vbs_all) else None

        def pick_phys(vb):
            bp = None
            bt = None
            nfree = 0
            for p in range(nphys):
                u = occ[p]
                if u is None:
                    t = 0.0
                elif u.nsched == len(u.ops):
                    t = u.lastfin
                else:
                    continue
                nfree += 1
                if bt is None or t < bt:
                    bt = t
                    bp = p
            if bp is not None and nfree < 1 + RESERVE and vb is not oldest_vb():
                return None, None
            return bp, bt

        while done < total:
            best = None
            for e in self.ENGS:
                lst = rem[e]
                if not lst:
                    continue
                cand = None
                fe = free[e]
                for k in range(min(window, len(lst))):
                    op = lst[k]
                    oid = id(op)
                    ready = rdy.get(oid)
                    if ready is None:
                        b_ = blk.get(oid)
                        if b_ is not None and b_.fin is None:
                            continue
                        ready = 0.0
                        ok = True
                        for p in op.deps:
                            if p.fin is None:
                                blk[oid] = p
                                ok = False
                                break
                            t = p.fin + lat
                            if t > ready:
                                ready = t
                        if ok:
                            for p in op.odeps:
                                if p.fin is None:
                                    blk[oid] = p
                                    ok = False
                                    break
                                if p.fin > ready:
                                    ready = p.fin
                        if not ok:
                            continue
                        rdy[oid] = ready
                    newvb = False
                    for vb in op.vbs:
                        if vb.phys is None:
                            newvb = True
                            bp, bt = pick_phys(vb)
                            if bp is None:
                                ready = None
                            elif bt + lat > ready:
                                ready = bt + lat
                            break
                    if ready is None:
                        continue
                    st = ready if ready > fe else fe
                    if cand is None or st < cand[0] - 1e-9:
                        cand = (st, k, op)
                    if st <= fe:
                        break
                if cand is not None and (best is None or cand[0] < best[0][0]):
                    best = (cand, e)
            (st, k, op), e = best
            rem[e].pop(k)
            out[e].append(op)
            for vb in op.vbs:
                if vb.phys is None:
                    bp, bt = pick_phys(vb)
                    u = occ[bp]
                    if u is not None:
                        last = {}
                        for q in u.ops:
                            kk = q.eng if q.dma is None else ("d", id(q))
                            if kk not in last or q.fin > last[kk].fin:
                                last[kk] = q
                        for q in last.values():
                            if q.eng == "pe" and op.eng == "pe" and q.dma is None:
                                op.odeps.append(q)
                            else:
                                op.deps.append(q)
                    occ[bp] = vb
                    vb.phys = bp
            op.fin = st + op.cost
            for vb in op.vbs:
                vb.nsched += 1
                if op.fin > vb.lastfin:
                    vb.lastfin = op.fin
            if op.dma is not None:
                free[e] = st + issue.get(e, 100.0)
            else:
                free[e] = op.fin
            done += 1
        self.ops = out
        self.est_ns = max(o.fin for e in self.ENGS for o in out[e])
        return self.est_ns


def build_program():
    nc = bass.Bass("TRN2", target_bir_lowering=False)
    S = Sched()

    def din(name, shape):
        return nc.dram_tensor(name, list(shape), F32, kind="ExternalInput").ap()

    xw = din("xw", (NCH, W, D))
    w_in = din("w_in", (D, DIN))
    wba = din("wba", (512, D))
    wbb = din("wbb", (512, D))
    wout = din("wout", (D, D))
    ngd = din("ng", (128, 8))
    gcd = din("gcols", (128, 4))
    ropec = din("ropec", (NCH, 128, W))
    ropes = din("ropes", (NCH, 128, W))
    natab = din("natab", (4, 4, 128, 1920))
    vfld = din("vfl", (NCH, 128, NVF))
    cmat = din("cmat", (3, 128, 128))
    bmask = din("bmask", (128, 512))
    yo = nc.dram_tensor("yo", [NCH, NQ, D], F32, kind="ExternalOutput").ap()

    import contextlib
    es = contextlib.ExitStack()

    def sb(name, cols, dt):
        return es.enter_context(nc.sbuf_tensor(name, [128, cols], dt))

    def psb(name, cols, dt):
        return es.enter_context(nc.psum_tensor(name, [128, cols], dt))

    with es:
        hT = sb("hT", 8 * W, BF16)
        og = sb("og", 2 * 4 * NQ, BF16)
        NSLOT = 6
        Wr = sb("Wr", NSLOT * 1024, BF16)
        AR = sb("AR", 16384, BF16)
        RT = sb("RT", 8192, BF16)
        NAT = sb("NAT", 3840, BF16)
        XA = sb("XA", 4096, F32)
        PT = sb("PT", 4 * 512, BF16)
        sq = sb("sq", NSC * 512, BF16)
        rs = sb("rs", NSC * 512, F32)
        qn = sb("qn", NSC * 512, BF16)
        t1 = sb("t1", NSC * 512, F32)
        t2 = sb("t2", NSC * 512, F32)
        xn = sb("xn", 2 * 1024, BF16)
        st_ = sb("stats", 16, F32)
        ng = sb("ngs", 8, F32)
        gc = sb("gcs", 4, F32)
        cm = sb("cm", 3 * 128, BF16)
        bm = sb("bm", 512, BF16)
        vfl = sb("vfls", NVF, F32)

        psT = psb("psT", 512, F32)
        psA = [psb("psA%d" % i, 512, F32) for i in range(2)]
        psX = [psb("psX%d" % i, 512, F32) for i in range(2)]
        psS = [psb("psS%d" % i, 512, F32) for i in range(2)]
        psO = psb("psO", 512, F32)
        psO0 = psO

        FREE = {"hT": 8 * W, "og": 2 * 4 * NQ, "Wr": NSLOT * 1024, "AR": 16384, "RT": 8192, "XA": 4096, "NAT": 3840,
                "PT": 2048, "sq": NSC * 512, "rs": NSC * 512, "qn": NSC * 512, "t1": NSC * 512, "t2": NSC * 512, "junk": 1024,
                "xn": 2048, "stats": 16, "ngs": 8, "gcs": 4, "cm": 384, "bm": 512, "vfls": NVF,
                "psT": 1024, "psO": 512}
        for i in range(2):
            FREE["psA%d" % i] = 512
            FREE["psX%d" % i] = 512
            FREE["psS%d" % i] = 512
        TN = {}

        def V(t, off, dims, p0=0, pn=128):
            if isinstance(t, VB):
                return LazyAP(t, off, dims, p0, pn)
            Fz = FREE[t.name]
            return bass.AP(t, p0 * Fz + off, [[Fz, pn]] + [[s, c] for s, c in dims])

        def RS(x):
            if isinstance(x, LazyAP):
                t = banks[x.vb.phys]
                if x.bf is not None:
                    vw = t[:, :].bitcast(BF16)
                    if x.bf[0] == "cols":
                        return vw[:, x.bf[1]:x.bf[2]]
                    return vw[:, 0:1024].rearrange("p (c t) -> p c t", c=8)
                return bass.AP(t, x.p0 * 512 + x.off, [[512, x.pn]] + [[s_, c_] for s_, c_ in x.dims])
            return x

        def VT(vb, a, b):
            return LazyAP(vb, 0, [(1, b - a)], 0, 128, bf=("cols", a, b))

        def VT8(vb):
            return LazyAP(vb, 0, [(128, 8), (1, 128)], 0, 128, bf=("c8",))

        def nfree(ap):
            n = 1
            for d_ in ap.shape[1:]:
                n *= d_
            return n

        def _is_fast(ap):
            return (not isinstance(ap, LazyAP)) and ap.dtype == BF16 and ap.ap[-1][0] == 1

        def _strided(ap):
            return (not isinstance(ap, LazyAP)) and ap.ap[-1][0] not in (0, 1)

        def vcost(eng, out, *ins):
            n = nfree(out)
            if eng != "dve":
                return 150.0 + 1.9 * n
            aps = [a_ for a_ in (out,) + ins if hasattr(a_, "shape")]
            if all(_is_fast(a_) for a_ in aps):
                return 120.0 + 0.55 * n
            if any(_strided(a_) for a_ in aps):
                return 150.0 + 1.9 * n
            return 100.0 + 1.15 * n

        def mm(out, lhsT, rhs, start, stop, reads, writes):
            S.emit("pe", lambda e: e.matmul(RS(out), RS(lhsT), RS(rhs), start=start, stop=stop), reads, writes,
                   cost=8.0 + 0.43 * max(nfree(out), 100))

        def tr(out, in_, ident, reads, writes):
            S.emit("pe", lambda e: e.transpose(RS(out), in_, ident), reads, writes, cost=70.0)

        def act(out, in_, func, reads, writes, scale=1.0, bias=0.0, accum=None):
            c = 170.0 + 0.82 * nfree(out)
            if accum is None:
                S.emit("act", lambda e: e.activation(RS(out), RS(in_), func, bias=bias, scale=scale), reads, writes, cost=c)
            else:
                c = 170.0 + 0.82 * nfree(in_)
                S.emit("act", lambda e: e.activation(RS(out), RS(in_), func, bias=bias, scale=scale, accum_out=accum),
                       reads, writes, cost=c)

        def tt(eng, out, in0, in1, op, reads, writes):
            S.emit(eng, lambda e: e.tensor_tensor(RS(out), RS(in0), RS(in1), op), reads, writes, cost=vcost(eng, out, in0, in1))

        def ts(eng, out, in0, s1, s2, op0, op1, reads, writes):
            if s2 is None:
                S.emit(eng, lambda e: e.tensor_scalar(RS(out), RS(in0), s1, None, op0), reads, writes, cost=vcost(eng, out))
            else:
                S.emit(eng, lambda e: e.tensor_scalar(RS(out), RS(in0), s1, s2, op0, op1), reads, writes,
                       cost=vcost(eng, out))

        def stt(eng, out, in0, scalar, in1, op0, op1, reads, writes):
            S.emit(eng, lambda e: e.scalar_tensor_tensor(RS(out), RS(in0), scalar, RS(in1), op0, op1), reads, writes,
                   cost=vcost(eng, out))

        def recip(out, in_, reads, writes):
            S.emit("dve", lambda e: e.reciprocal(out, in_), reads, writes, cost=100.0 + 5.5 * nfree(out))

        def cp(eng, out, in_, reads, writes):
            S.emit(eng, lambda e: e.tensor_copy(RS(out), RS(in_)), reads, writes, cost=vcost(eng, out))

        pool_dmas = []

        def dma(q, out, in_, reads, writes, key):
            op = S.emit(q, lambda e: e.dma_start(out=out, in_=in_), reads, writes, dma=key,
                        cost=2500.0 + 0.02 * 128 * nfree(out) if out.shape[0] == 128 else 4000.0)
            if q == "pool":
                if len(pool_dmas) >= MAX_SWDGE_INFLIGHT:
                    op.deps.append(pool_dmas[-MAX_SWDGE_INFLIGHT])
                pool_dmas.append(op)

        A = S.A
        R = S.R

        dma("pool", V(cm, 0, [(128, 3), (1, 128)]), cmat.rearrange("k p c -> p k c"), [], [A("cm")], "c0")
        dma("pool", V(bm, 0, [(1, 512)]), bmask, [], [A("bm")], "c1")
        dma("sp", V(ng, 0, [(1, 8)]), ngd, [], [A("ng")], "c2")
        dma("sp", V(gc, 0, [(1, 4)]), gcd, [], [A("gc")], "c3")
        ts("dve", V(gc, 0, [(2, 2)]), V(gc, 0, [(2, 2)]), 0.125, None, ALU.mult, None, [A("gc")], [A("gc")])
        IDENT = V(cm, 0, [(1, 128)])
        PERM = V(cm, 128, [(1, 128)])
        BD = V(cm, 256, [(1, 128)])

        wstate = {"n": 0}

        def load_w(src_ap_fn, nchunk=8, ncol=128):
            s = wstate["n"] % NSLOT
            wstate["n"] += 1
            dma("pool", V(Wr, s * 1024, [(ncol, nchunk), (1, ncol)]), src_ap_fn, [], [A("W", s)], "w%d" % s)
            return s

        w_in_v = w_in.rearrange("(c p) e -> p c e", p=128)

        def w_in_cols(col0, ncol=128):
            return w_in_v[:, :, col0:col0 + ncol]

        def hT_tiles(t0, t1_):
            return [A("hT", i) for i in range(t0 // 128, (t1_ - 1) // 128 + 1)]

        acc_rr = {"n": 0}
        aux_rr = {"n": 0}
        sc_rr = {"n": 0}
        st_rr = {"n": 0}
        pt_rr = {"n": 0}

        banks = [psA[0], psA[1], psX[0], psX[1], psS[0], psS[1], psO0, psT]

        def next_bank():
            vb = VB()
            S.vbs_all.append(vb)
            return vb, vb.atom

        next_acc = next_bank
        next_aux = next_bank
        next_st = next_bank

        def proj_fm(slot, t0, N):
            ps, pa = next_acc()
            for c in range(8):
                mm(V(ps, 0, [(1, N)]), V(Wr, slot * 1024 + c * 128, [(1, 128)]),
                   V(hT, c * W + t0, [(1, N)]), c == 0, c == 7,
                   [A("W", slot)] + hT_tiles(t0, t0 + N), [pa])
            return ps, pa

        def rsqrt_chain(ps, pa, N, sl):
            o = sl * 512
            act(V(sq, o, [(1, N)]), V(ps, 0, [(1, N)]), AF.Square, [pa], [A("sq", sl)])
            px, pxa = next_aux()
            mm(V(px, 0, [(1, N)]), BD, V(sq, o, [(1, N)]), True, True, [A("cm"), A("sq", sl)], [pxa])
            act(V(rs, o, [(1, N)]), V(px, 0, [(1, N)]), AF.Ln, [pxa], [A("rs", sl)], scale=1.0, bias=EPS)
            act(V(rs, o, [(1, N)]), V(rs, o, [(1, N)]), AF.Exp, [A("rs", sl)], [A("rs", sl)], scale=-0.5)

        def qk_tile(slot, t0, N, gcol, dst_off, d, sub, n_off, cidx):
            ps, pa = proj_fm(slot, t0, N)
            sl = sc_rr["n"] % NSC
            sc_rr["n"] += 1
            o = sl * 512
            rsqrt_chain(ps, pa, N, sl)
            if d == 0:
                c0 = dst_off + (t0 - n_off)
                stt("dve", V(AR, c0, [(1, N)]), V(ps, 0, [(1, N)]), V(gc, gcol, [(1, 1)]), V(rs, o, [(1, N)]),
                    ALU.mult, ALU.mult, [pa, A("gc"), A("rs", sl)], R("AR", c0, c0 + N))
                return
            stt("dve", V(qn, o, [(1, N)]), V(ps, 0, [(1, N)]), V(gc, gcol, [(1, 1)]), V(rs, o, [(1, N)]),
                ALU.mult, ALU.mult, [pa, A("gc"), A("rs", sl)], [A("qn", sl)])
            px, pxa = next_aux()
            mm(V(px, 0, [(1, N)]), PERM, V(qn, o, [(1, N)]), True, True, [A("cm"), A("qn", sl)], [pxa])
            tt("pool", V(t1, o, [(1, N)]), V(qn, o, [(1, N)]), V(RT, t0, [(1, N)]), ALU.mult,
               [A("qn", sl)] + R("RT", t0, t0 + N), [A("t1", sl)])
            tt("dve", V(t2, o, [(1, N)]), V(px, 0, [(1, N)]), V(RT, W + t0, [(1, N)]), ALU.mult,
               [pxa] + R("RT", W + t0, W + t0 + N), [A("t2", sl)])
            n0 = t0 // d - n_off
            cnt = N // d
            wr = []
            for r in range(d):
                wr += R("AR", dst_off + r * sub + n0, dst_off + r * sub + n0 + cnt)
            tt("pool" if (ROPE_ADD_POOL and (sc_rr["n"] % ROPE_ADD_POOL == 0)) else "dve", V(AR, dst_off + n0, [(sub, d), (1, cnt)]),
               V(t1, o, [(1, d), (d, cnt)]), V(t2, o, [(1, d), (d, cnt)]), ALU.add,
               [A("t1", sl), A("t2", sl)], wr)

        QT, KT, VV, SG = 0, 2048, 6144, 14336

        def v_blocks(slot, nblk, tok_fn, d, vf_col):
            cp("dve", V(AR, VV + 64, [(256, nblk), (64, 2), (1, 64)]),
               V(vfl, vf_col, [(1, nblk), (0, 2), (0, 64)]), [A("vfl")], [A("VFLG")])
            for b0 in range(0, nblk, 4):
                nb = min(4, nblk - b0)
                ps, pa = next_acc()
                for jj in range(nb):
                    tk = tok_fn(b0 + jj)
                    for c in range(8):
                        mm(V(ps, jj * 128, [(1, 128)]), V(hT, c * W + tk, [(d, 128)]),
                           V(Wr, slot * 1024 + c * 128, [(1, 128)]), c == 0, c == 7,
                           [A("W", slot)] + hT_tiles(tk, tk + 127 * d + 1), [pa])
                c0 = VV + b0 * 256
                wr = R("AR", c0, c0 + nb * 256)
                cp("dve", V(AR, c0, [(256, nb), (192, 2), (1, 64)]), V(ps, 0, [(128, nb), (64, 2), (1, 64)]),
                   [pa], wr)

        def gate_tiles(slot):
            for i in range(4):
                t0 = Q0 + 512 * i
                ps, pa = proj_fm(slot, t0, 512)
                sl = sc_rr["n"] % NSC
                sc_rr["n"] += 1
                o = sl * 512
                act(V(t1, o, [(1, 512)]), V(ps, 0, [(1, 512)]), AF.Exp, [pa], [A("t1", sl)], scale=-1.0)
                act(V(t1, o, [(1, 512)]), V(t1, o, [(1, 512)]), AF.Ln, [A("t1", sl)], [A("t1", sl)], bias=1.0)
                act(V(t1, o, [(1, 512)]), V(t1, o, [(1, 512)]), AF.Exp, [A("t1", sl)], [A("t1", sl)], scale=-1.0)
                c0 = SG + 512 * i
                tt("dve", V(AR, c0, [(1, 512)]), V(ps, 0, [(1, 512)]), V(t1, o, [(1, 512)]), ALU.mult,
                   [pa, A("t1", sl)], R("AR", c0, c0 + 512))

        mul_rr = {"n": 0}
        nat_rr = {"n": 0}

        def mul_eng():
            return "dve"

        OUTER = list(range(0, 8)) + list(range(24, 32))
        INNER = list(range(8, 24))

        def phase_a(ch, tiles):
            for ti in tiles:
                xs = ti % 4
                dma("sp", V(XA, xs * 1024, [(1, 1024)]), xw[ch, ti * 128:(ti + 1) * 128, :], [], [A("XA", xs)],
                    "x%d" % xs)
                ss = ti % 2
                act(V(xn, ss * 1024, [(1, 1024)]), V(XA, xs * 1024, [(1, 1024)]), AF.Square, [A("XA", xs)],
                    [A("xn", ss), A("st", ss)], accum=V(st_, ss, [(1, 1)]))
                act(V(st_, 2 + ss, [(1, 1)]), V(st_, ss, [(1, 1)]), AF.Ln, [A("st", ss)], [A("st2", ss)],
                    scale=1.0 / D, bias=EPS)
                act(V(st_, 4 + ss, [(1, 1)]), V(st_, 2 + ss, [(1, 1)]), AF.Exp, [A("st2", ss)], [A("st3", ss)],
                    scale=-0.5)
                ts("dve", V(xn, ss * 1024, [(1, 1024)]), V(XA, xs * 1024, [(1, 1024)]), V(st_, 4 + ss, [(1, 1)]),
                   None, ALU.mult, None, [A("XA", xs), A("st3", ss)], [A("xn", ss)])
                vbT, vbTa = next_bank()
                for c in range(8):
                    tr(VT(vbT, c * 128, (c + 1) * 128), V(xn, ss * 1024 + c * 128, [(1, 128)]), IDENT,
                       [A("xn", ss), A("cm")], [vbTa])
                tt("dve", V(hT, ti * 128, [(W, 8), (1, 128)]), VT8(vbT),
                   V(ng, 0, [(1, 8), (0, 128)]), ALU.mult, [vbTa, A("ng")], [A("hT", ti)])

        for ch in range(NCH):
            if ch == 0:
                dma("sp", V(vfl, 0, [(1, NVF)]), vfld[ch], [], [A("vfl")], "vf")
                phase_a(ch, INNER)
                phase_a(ch, OUTER)
            else:
                phase_a(ch, INNER)

            dma("pool", V(RT, 0, [(1, W)]), ropec[ch], [], R("RT", 0, W), "rc")
            dma("pool", V(RT, W, [(1, W)]), ropes[ch], [], R("RT", W, 2 * W), "rs")
            for jp in range(4):
                cb = jp * 128
                if ch == 0:
                    tv = (1, 0, 0)
                elif ch == 1:
                    tv = (0, 0, 1)
                else:
                    tv = (2, 0, 3)
                s_q = load_w(w_in_cols(QA + cb))
                s_k = load_w(w_in_cols(KA + cb))
                s_v = load_w(w_in_cols(VA + cb))
                s_g = load_w(w_in_cols(GA + cb))
                for i in range(4):
                    qk_tile(s_q, Q0 + 512 * i, 512, 0, QT, 0, 0, Q0, 0)
                for i in range(5):
                    qk_tile(s_k, 768 + 512 * i, 512, 1, KT, 0, 0, 0, 0)
                v_blocks(s_v, 20, lambda b: 768 + 128 * b, 1, VF_OFF["na"])
                gate_tiles(s_g)
                def load_tab(var, h2, dst):
                    xs = nat_rr["n"] % 4
                    nat_rr["n"] += 1
                    dma("sp", V(XA, xs * 1024, [(1, 960)]), natab[jp, var, :, h2 * 960:(h2 + 1) * 960],
                        [], [A("XA", xs)], "x%d" % xs)
                    act(V(NAT, dst * 960, [(1, 960)]), V(XA, xs * 1024, [(1, 960)]), AF.Exp, [A("XA", xs)],
                        [A("NAT", dst)])
                for h2 in range(2):
                    load_tab(tv[1], h2, h2)
                    load_tab(tv[0], h2, 2 + h2)
                for hh in range(2):
                    pq = hh * 64
                    pd = (1 - hh) * 64
                    for rg in range(8):
                        if rg == 7:
                            load_tab(tv[2], hh, 2 + hh)
                        tslot = (2 + hh) if rg in (0, 7) else hh
                        psO, psOa = next_bank()
                        for pp in range(3):
                            stp, sta = next_st()
                            for slot_i, kb in ((0, 2 * pp + 1), (1, 2 * pp)):
                                j = 2 * rg + kb
                                kc = KT + 768 + 128 * j
                                mm(V(stp, slot_i * 256, [(1, 256)]), V(AR, kc, [(1, 128)], pq, 64),
                                   V(AR, QT + 256 * rg, [(1, 256)], pq, 64), True, True,
                                   R("AR", kc, kc + 128) + R("AR", QT + 256 * rg, QT + 256 * rg + 256), [sta])
                            pts = pt_rr["n"] % 4
                            pt_rr["n"] += 1
                            act(V(PT, pts * 512, [(1, 512)]), V(stp, 0, [(1, 512)]), AF.Exp, [sta], [A("PT", pts)])
                            tc0 = tslot * 960 + (9 - 4 * pp) * 64
                            tt(mul_eng(), V(PT, pts * 512, [(256, 2), (1, 256)]),
                               V(PT, pts * 512, [(256, 2), (1, 256)]), V(NAT, tc0, [(128, 2), (1, 256)]), ALU.mult,
                               [A("PT", pts), A("NAT", tslot)], [A("PT", pts)])
                            for slot_i, kb in ((0, 2 * pp + 1), (1, 2 * pp)):
                                j = 2 * rg + kb
                                vc = VV + j * 256 + hh * 128
                                first = (pp == 0 and slot_i == 0)
                                last = (pp == 2 and slot_i == 1)
                                mm(V(psO, 0, [(1, 256)]), V(AR, vc, [(1, 128)]),
                                   V(PT, pts * 512 + slot_i * 256, [(1, 256)]), first, last,
                                   R("AR", vc, vc + 128) + [A("PT", pts), A("VFLG")], [psOa])
                        sl = sc_rr["n"] % NSC
                        sc_rr["n"] += 1
                        o = sl * 512
                        act(V(rs, o, [(1, 256)], pq, 64), V(psO, 0, [(1, 256)], pd, 64), AF.Ln, [psOa],
                            [A("rs", sl)])
                        act(V(rs, o, [(1, 256)], pq, 64), V(rs, o, [(1, 256)], pq, 64), AF.Exp, [A("rs", sl)],
                            [A("rs", sl)], scale=-1.0)
                        tt("dve", V(t1, o, [(1, 256)], pq, 64), V(psO, 0, [(1, 256)], pq, 64),
                           V(rs, o, [(1, 256)], pq, 64), ALU.mult, [psOa, A("rs", sl)], [A("t1", sl)])
                        oc = (0 * 4 + jp) * NQ + 256 * rg
                        tt(FIN_ENG, V(og, oc, [(1, 256)], pq, 64), V(t1, o, [(1, 256)], pq, 64),
                           V(AR, SG + 256 * rg, [(1, 256)], pq, 64), ALU.mult,
                           [A("t1", sl)] + R("AR", SG + 256 * rg, SG + 256 * rg + 256),
                           [A("og", 0, jp, rg // 2)])

                s_g = load_w(w_in_cols(GB + cb))
                gate_tiles(s_g)
                for gi, (d, qc, kc_, vc_) in enumerate(DIL):
                    sub = W // d
                    qsub = NQ // d
                    nj = 16 // d + 1
                    s_q = load_w(w_in_cols(qc + cb))
                    s_k = load_w(w_in_cols(kc_ + cb))
                    s_v = load_w(w_in_cols(vc_ + cb))
                    for i in range(4):
                        qk_tile(s_q, Q0 + 512 * i, 512, 2, QT, d, qsub, Q0 // d, 0)
                    if d == 1:
                        ktiles = [(960 + 512 * i, 512) for i in range(4)] + [(3008, 128)]
                    elif d == 4:
                        ktiles = [(768 + 512 * i, 512) for i in range(5)]
                    else:
                        ktiles = [(512 * i, 512) for i in range(8)]
                    for (t0, N) in ktiles:
                        qk_tile(s_k, t0, N, 3, KT, d, sub, 0, 0)
                    kbase = Q0 // d - 64

                    def tokfn(b, d=d, nj=nj, kbase=kbase):
                        r, j = divmod(b, nj)
                        return (kbase + 128 * j) * d + r
                    v_blocks(s_v, d * nj, tokfn, d, VF_OFF[d])
                    nb_r = 16 // d
                    for hh in range(2):
                        pq = hh * 64
                        for qp in range(8):
                            stp, sta = next_st()
                            psO, psOa = next_bank()
                            info = []
                            for qi in range(2):
                                qb = 2 * qp + qi
                                r, b = divmod(qb, nb_r)
                                info.append((r, b))
                                qcol = QT + r * qsub + 128 * b
                                for half in range(2):
                                    j = b + half
                                    kcol = KT + r * sub + kbase + 128 * j
                                    mm(V(stp, qi * 256 + half * 128, [(1, 128)]), V(AR, kcol, [(1, 128)], pq, 64),
                                       V(AR, qcol, [(1, 128)], pq, 64), True, True,
                                       R("AR", kcol, kcol + 128) + R("AR", qcol, qcol + 128), [sta])
                            pts = pt_rr["n"] % 4
                            pt_rr["n"] += 1
                            act(V(PT, pts * 512, [(1, 512)]), V(stp, 0, [(1, 512)]), AF.Exp, [sta], [A("PT", pts)])
                            tt(mul_eng(), V(PT, pts * 512, [(1, 512)]), V(PT, pts * 512, [(1, 512)]),
                               V(bm, 0, [(1, 512)]), ALU.mult, [A("PT", pts), A("bm")], [A("PT", pts)])
                            for qi in range(2):
                                r, b = info[qi]
                                for half in range(2):
                                    kbi = r * nj + b + half
                                    vc = VV + kbi * 256 + hh * 128
                                    mm(V(psO, qi * 128, [(1, 128)]), V(AR, vc, [(1, 128)]),
                                       V(PT, pts * 512 + qi * 256 + half * 128, [(1, 128)]), half == 0, half == 1,
                                       R("AR", vc, vc + 128) + [A("PT", pts), A("VFLG")], [psOa])
                            (r0, b0), (r1, b1) = info
                            off0 = r0 + 128 * b0 * d
                            off1 = r1 + 128 * b1 * d
                            accv = V(XA, hh * 2048 + off0, [(off1 - off0, 2), (d, 128)])
                            acca = [A("XA", 2 * hh), A("XA", 2 * hh + 1)]
                            if gi == 0:
                                cp("dve", accv, V(psO, 0, [(128, 2), (1, 128)]), [psOa], acca)
                            else:
                                tt("dve", accv, V(psO, 0, [(128, 2), (1, 128)]), accv, ALU.add,
                                   [psOa] + acca, acca)
                for hh in range(2):
                    pq = hh * 64
                    pd = (1 - hh) * 64
                    for i in range(4):
                        sl = sc_rr["n"] % NSC
                        sc_rr["n"] += 1
                        o = sl * 512
                        acca = [A("XA", 2 * hh), A("XA", 2 * hh + 1)]
                        act(V(rs, o, [(1, 512)], pq, 64), V(XA, hh * 2048 + 512 * i, [(1, 512)], pd, 64), AF.Ln,
                            acca, [A("rs", sl)])
                        act(V(rs, o, [(1, 512)], pq, 64), V(rs, o, [(1, 512)], pq, 64), AF.Exp, [A("rs", sl)],
                            [A("rs", sl)], scale=-1.0)
                        tt("dve", V(t1, o, [(1, 512)], pq, 64), V(XA, hh * 2048 + 512 * i, [(1, 512)], pq, 64),
                           V(rs, o, [(1, 512)], pq, 64), ALU.mult, acca + [A("rs", sl)], [A("t1", sl)])
                        oc = (1 * 4 + jp) * NQ + 512 * i
                        tt(FIN_ENG, V(og, oc, [(1, 512)], pq, 64), V(t1, o, [(1, 512)], pq, 64),
                           V(AR, SG + 512 * i, [(1, 512)], pq, 64), ALU.mult,
                           [A("t1", sl)] + R("AR", SG + 512 * i, SG + 512 * i + 512), [A("og", 1, jp, i)])

            if ch + 1 < NCH:
                dma("sp", V(vfl, 0, [(1, NVF)]), vfld[ch + 1], [], [A("vfl")], "vf")
                phase_a(ch + 1, OUTER)
            dma("pool", V(RT, 0, [(1024, 8), (1, 1024)]), wout.rearrange("(c p) e -> p c e", p=128), [],
                R("RT", 0, 8192), "wo")
            for ec in range(8):
                s_ma = load_w(w_in_cols(MA + ec * 128))
                s_mb = load_w(w_in_cols(MB + ec * 128))
                s_ba = load_w(wba.rearrange("(c p) e -> p c e", p=128)[:, :, ec * 128:(ec + 1) * 128], nchunk=4)
                s_bb = load_w(wbb.rearrange("(c p) e -> p c e", p=128)[:, :, ec * 128:(ec + 1) * 128], nchunk=4)
                for tq in range(4):
                    t0 = Q0 + 512 * tq
                    pma, pmaa = proj_fm(s_ma, t0, 512)
                    pmb, pmba = proj_fm(s_mb, t0, 512)
                    pba, pbaa = next_aux()
                    pbb, pbba = next_aux()
                    for ab, (pb_, pba_, sslot) in enumerate(((pba, pbaa, s_ba), (pbb, pbba, s_bb))):
                        for jp in range(4):
                            mm(V(pb_, 0, [(1, 512)]), V(Wr, sslot * 1024 + jp * 128, [(1, 128)]),
                               V(og, (ab * 4 + jp) * NQ + 512 * tq, [(1, 512)]), jp == 0, jp == 3,
                               [A("W", sslot), A("og", ab, jp, tq)], [pba_])
                    sl = sc_rr["n"] % NSC
                    sc_rr["n"] += 1
                    o = sl * 512
                    for (pm, pma_, pb_, pba_, tb, tbn) in ((pma, pmaa, pba, pbaa, t1, "t1"),
                                                          (pmb, pmba, pbb, pbba, t2, "t2")):
                        act(V(tb, o, [(1, 512)]), V(pm, 0, [(1, 512)]), AF.Exp, [pma_], [A(tbn, sl)], scale=-1.0)
                        act(V(tb, o, [(1, 512)]), V(tb, o, [(1, 512)]), AF.Ln, [A(tbn, sl)], [A(tbn, sl)], bias=1.0)
                        act(V(tb, o, [(1, 512)]), V(tb, o, [(1, 512)]), AF.Exp, [A(tbn, sl)], [A(tbn, sl)],
                            scale=-1.0)
                        tt("dve", V(tb, o, [(1, 512)]), V(pb_, 0, [(1, 512)]), V(tb, o, [(1, 512)]), ALU.mult,
                           [pba_, A(tbn, sl)], [A(tbn, sl)])
                    mc = ec * NQ + 512 * tq
                    tt("pool" if MERGE_ADD_POOL else "dve", V(AR, mc, [(1, 512)]), V(t1, o, [(1, 512)]), V(t2, o, [(1, 512)]), ALU.add,
                       [A("t1", sl), A("t2", sl)], R("AR", mc, mc + 512) + [A("VFLG")])
            for ti in range(16):
                xs = ti % 4
                tok = Q0 + 128 * ti
                dma("sp", V(XA, xs * 1024, [(1, 1024)]), xw[ch, tok:tok + 128, :], [], [A("XA", xs)], "x%d" % xs)
                for half in range(2):
                    stp, sta = next_st()
                    for ec in range(8):
                        mc = ec * NQ + 128 * ti
                        mm(V(stp, 0, [(1, 512)]), V(AR, mc, [(1, 128)]),
                           V(RT, ec * 1024 + half * 512, [(1, 512)]), ec == 0, ec == 7,
                           R("AR", mc, mc + 128) + R("RT", ec * 1024 + half * 512, ec * 1024 + half * 512 + 512)
                           + [A("VFLG")], [sta])
                    tt("dve", V(XA, xs * 1024 + half * 512, [(1, 512)]), V(stp, 0, [(1, 512)]),
                       V(XA, xs * 1024 + half * 512, [(1, 512)]), ALU.add, [sta, A("XA", xs)], [A("XA", xs)])
                dma("sp", yo[ch, 128 * ti:128 * (ti + 1), :], V(XA, xs * 1024, [(1, 1024)]), [A("XA", xs)], [],
                    "x%d" % xs)

        if DO_SCHED:
            S.schedule(window=WINDOW, nphys=NPHYS)
        else:
            S.est_ns = 0
        print('[sched] est_us=%.1f ops=%s' % (S.est_ns / 1e3, {e: len(S.ops[e]) for e in S.ENGS}), flush=True)
        pos = {}
        for e in S.ENGS:
            for i_, op in enumerate(S.ops[e]):
                pos[id(op)] = i_
        waits_of = {}
        for e in S.ENGS:
            seen = {}
            for op in S.ops[e]:
                grp = {}
                for p in op.deps:
                    kk = p.eng if p.dma is None else ("d", p.dma)
                    q = grp.get(kk)
                    if q is None or pos[id(p)] > pos[id(q)]:
                        grp[kk] = p
                wl = []
                for kk, p in grp.items():
                    if seen.get(kk, -1) >= pos[id(p)]:
                        continue
                    seen[kk] = pos[id(p)]
                    p.needed = True
                    wl.append(p)
                waits_of[id(op)] = wl
        sems = {}
        for e in S.ENGS:
            sems[e] = es.enter_context(nc.semaphore("sem_" + e))
        dsems = {}
        for k in S.dma_keys:
            dsems[k] = es.enter_context(nc.semaphore("dsem_" + k))
        dcount = {k: 0 for k in S.dma_keys}
        nsig = {}
        for e in S.ENGS:
            cnt = 0
            for op in S.ops[e]:
                if op.dma is not None:
                    continue
                if op.needed:
                    cnt += 1
                    op.sig = (sems[e], cnt)
            nsig[e] = cnt
        for e in S.ENGS:
            for op in S.ops[e]:
                if op.dma is not None:
                    dcount[op.dma] += 16
                    op.sig = (dsems[op.dma], dcount[op.dma])
        print('[sync] signalling ops per engine:', nsig, flush=True)

        def run_engine(ename, eh):
            for op in S.ops[ename]:
                for p in waits_of[id(op)]:
                    eh.wait_ge(p.sig[0], p.sig[1])
                ins = op.fn(eh)
                if op.dma is not None:
                    ins.then_inc(op.sig[0], 16)
                elif op.needed:
                    ins.then_inc(op.sig[0], 1)
            if ename == "sp":
                for k, v in dcount.items():
                    if v > 0:
                        eh.wait_ge(dsems[k], v)

        with nc.Block() as block:
            @block.tensor
            def _(e):
                run_engine("pe", e)

            @block.scalar
            def _(e):
                run_engine("act", e)

            @block.vector
            def _(e):
                run_engine("dve", e)

            @block.gpsimd
            def _(e):
                run_engine("pool", e)

            @block.sync
            def _(e):
                run_engine("sp", e)
    return nc


def _na_tables(rel_bias):
    rb = np.asarray(rel_bias, np.float32)
    p = np.arange(128)
    half = p // 64
    ck = p % 64
    s = np.arange(15)
    cq = np.arange(64)
    drow = 7 - s[None, :] + half[:, None]
    cstart = np.clip(cq - 8, 0, 48)
    colok = (ck[:, None] >= cstart[None, :]) & (ck[:, None] < cstart[None, :] + 16)
    dcol = np.clip(ck[:, None] - cq[None, :], -15, 15) + 15
    out = np.full((4, 4, 128, 2, 15, 64), -30000.0, np.float32)
    for var in range(2):
        if var == 0:
            rowok = (drow >= -4) & (drow <= 3)
        else:
            rowok = (drow >= -7) & (drow <= 7)
        ok = rowok[:, :, None] & colok[:, None, :]
        ridx = np.clip(drow + 7, 0, 14)
        for h in range(8):
            vals = rb[h][ridx[:, :, None], dcol[:, None, :]]
            out[h // 2, var, :, h % 2] = np.where(ok, vals, np.float32(-30000.0))
    return out.reshape(4, 4, 128, 1920)


def _rope_tables(pos):
    inv = (10000.0 ** (-np.arange(0, 64, 2, dtype=np.float32) / np.float32(64))).astype(np.float32)
    ang = (pos.astype(np.float32)[None, :] * inv[:, None]).astype(np.float32)
    c = np.cos(ang.astype(np.float64)).astype(np.float32)
    s = np.sin(ang.astype(np.float64)).astype(np.float32)
    C = np.tile(c, (4, 1))
    Sg = np.concatenate([-s, s, -s, s], axis=0)
    return C, Sg


def _valid_flags(valid):
    out = np.zeros((128, NVF), np.float32)
    i = np.arange(128)
    for j in range(20):
        out[:, VF_OFF["na"] + j] = valid[768 + 128 * j + i]
    for d in (1, 4, 16):
        nj = 16 // d + 1
        kbase = Q0 // d - 64
        for r in range(d):
            for j in range(nj):
                out[:, VF_OFF[d] + r * nj + j] = valid[(kbase + 128 * j + i) * d + r]
    return out


_CACHE = {}


def kernel(x_prompt, x_sample, norm_gain, w_in, qn_a, kn_a, rel_bias_a, qn_b, kn_b, w_branch_a, w_branch_b, w_out):
    x_prompt = np.asarray(x_prompt, np.float32)
    x_sample = np.asarray(x_sample, np.float32)
    if "nc" not in _CACHE:
        _CACHE["nc"] = build_program()
    nc = _CACHE["nc"]

    w_in2 = np.ascontiguousarray(np.asarray(w_in, np.float32)[0])
    wba = np.ascontiguousarray(np.asarray(w_branch_a, np.float32)[0])
    wbb = np.ascontiguousarray(np.asarray(w_branch_b, np.float32)[0])
    wo = np.ascontiguousarray(np.asarray(w_out, np.float32)[0])
    ng = np.ascontiguousarray(np.asarray(norm_gain, np.float32)[0].reshape(8, 128).T)
    gcols = np.stack([np.tile(np.asarray(a, np.float32)[0], 2) for a in (qn_a, qn_b, kn_a, kn_b)], axis=1)
    gcols = np.ascontiguousarray(gcols[:, [0, 2, 1, 3]])
    nat = _na_tables(np.asarray(rel_bias_a, np.float32)[0])
    ident = np.eye(128, dtype=np.float32)
    perm = np.zeros((128, 128), np.float32)
    for m in range(128):
        perm[m, m + 32 if (m % 64) < 32 else m - 32] = 1.0
    bd = np.zeros((128, 128), np.float32)
    bd[:64, :64] = 1.0 / 64
    bd[64:, 64:] = 1.0 / 64
    cmat = np.stack([ident, perm, bd])
    kk = np.arange(128)[:, None]
    qq = np.arange(128)[None, :]
    m1 = np.concatenate([(qq <= kk), (qq >= kk)], axis=1).astype(np.float32)
    bmask = np.ascontiguousarray(np.concatenate([m1, m1], axis=1))

    in_maps = []
    for c in range(NCORES):
        xw = np.zeros((NCH, W, D), np.float32)
        vfl = np.zeros((NCH, 128, NVF), np.float32)
        rc = np.zeros((NCH, 128, W), np.float32)
        rsn = np.zeros((NCH, 128, W), np.float32)
        for ch, (src, L, q0) in enumerate(((x_prompt[c], 4096, 0), (x_prompt[c], 4096, 2048),
                                           (x_sample[0], 16384, 2048 * c))):
            lo = q0 - Q0
            pos = np.arange(lo, lo + W)
            valid = ((pos >= 0) & (pos < L))
            a = max(lo, 0)
            b = min(lo + W, L)
            xw[ch, a - lo:b - lo] = src[a:b]
            vfl[ch] = _valid_flags(valid.astype(np.float32))
            C, Sg = _rope_tables(np.clip(pos, 0, L - 1))
            rc[ch] = C
            rsn[ch] = Sg
        natc = nat.copy()
        natc[:, 2] = nat[:, 1] if c == 0 else nat[:, 0]
        natc[:, 3] = nat[:, 1] if c == NCORES - 1 else nat[:, 0]
        in_maps.append({"xw": xw, "w_in": w_in2, "wba": wba, "wbb": wbb, "wout": wo, "ng": ng, "gcols": gcols,
                        "ropec": rc, "ropes": rsn, "natab": natc, "vfl": vfl, "cmat": cmat, "bmask": bmask})

    res = run_bass_kernel_spmd(nc, in_maps, core_ids=list(range(NCORES)))
    y_prompt = np.empty((8, 4096, D), np.float32)
    y_sample = np.empty((1, 16384, D), np.float32)
    for c in range(NCORES):
        yo = res.results[c]["yo"]
        y_prompt[c, 0:2048] = yo[0]
        y_prompt[c, 2048:4096] = yo[1]
        y_sample[0, 2048 * c:2048 * (c + 1)] = yo[2]
    return (y_prompt, y_sample)
```

```python
import numpy as np
import concourse.bass as bass
import concourse.mybir as mybir
from concourse.bass_utils import run_bass_kernel_spmd

F32 = mybir.dt.float32
BF16 = mybir.dt.bfloat16
AF = mybir.ActivationFunctionType
ALU = mybir.AluOpType

NCORES = 8
NPHYS = 8
MAX_SWDGE_INFLIGHT = 3
LAT = 300.0
FIN_ENG = "dve"
ROPE_ADD_POOL = 0
MERGE_ADD_POOL = 0
NSC = 2
WINDOW = 256
DO_SCHED = True
D = 1024
DIN = 9216
W = 4096
Q0 = 1024
NQ = 2048
NCH = 3
EPS = 1e-6
QA, KA, VA, GA = 0, 512, 1024, 1536
DIL = ((1, 2048, 2560, 3072), (4, 3584, 4096, 4608), (16, 5120, 5632, 6144))
GB, MA, MB = 6656, 7168, 8192
NVF = 20 + 17 + 20 + 32
VF_OFF = {"na": 0, 1: 20, 4: 37, 16: 57}


class Atom:
    __slots__ = ("lw", "rd", "vb")

    def __init__(self):
        self.lw = None
        self.rd = {}
        self.vb = None


class VB:
    __slots__ = ("atom", "ops", "phys", "nsched", "lastfin")

    def __init__(self):
        self.atom = Atom()
        self.atom.vb = self
        self.ops = []
        self.phys = None
        self.nsched = 0
        self.lastfin = 0.0


class LazyAP:
    __slots__ = ("vb", "off", "dims", "p0", "pn", "shape", "bf")

    def __init__(self, vb, off, dims, p0, pn, bf=None):
        self.vb = vb
        self.off = off
        self.dims = dims
        self.p0 = p0
        self.pn = pn
        self.bf = bf
        self.shape = (pn,) + tuple(c for _, c in dims)


class Op:
    __slots__ = ("eng", "fn", "deps", "odeps", "dma", "needed", "sig", "idx", "cost", "fin", "gi", "vbs")

    def __init__(self, eng, fn, dma):
        self.eng = eng
        self.fn = fn
        self.dma = dma
        self.deps = []
        self.odeps = []
        self.needed = False
        self.sig = None
        self.cost = 300.0
        self.fin = None
        self.vbs = []


class Sched:
    ENGS = ("pe", "act", "dve", "pool", "sp")

    def __init__(self):
        self.ops = {e: [] for e in self.ENGS}
        self.atoms = {}
        self.dma_keys = {}
        self.gcount = 0
        self.vbs_all = []

    def A(self, *key):
        a = self.atoms.get(key)
        if a is None:
            a = self.atoms[key] = Atom()
        return a

    def R(self, name, a, b, g=128):
        return [self.A(name, i) for i in range(a // g, (b - 1) // g + 1)]

    def emit(self, eng, fn, reads=(), writes=(), dma=None, cost=300.0):
        op = Op(eng, fn, dma)
        op.idx = len(self.ops[eng])
        op.cost = cost
        op.gi = self.gcount
        self.gcount += 1
        deps = {}
        odeps = {}

        def add(p):
            if p is None:
                return
            if p.eng == "pe" and eng == "pe" and p.dma is None:
                odeps[id(p)] = p
                return
            deps[id(p)] = p

        for b in reads:
            add(b.lw)
        for b in writes:
            add(b.lw)
            for p in b.rd.values():
                add(p)
        op.deps = list(deps.values())
        op.odeps = list(odeps.values())
        for b in list(reads) + list(writes):
            if b.vb is not None and b.vb not in op.vbs:
                op.vbs.append(b.vb)
                b.vb.ops.append(op)
        rk = ("e", id(op))
        for b in reads:
            b.rd[rk] = op
        for b in writes:
            b.lw = op
            b.rd = {}
        self.ops[eng].append(op)
        if dma is not None:
            self.dma_keys.setdefault(dma, 0)
        return op

    def schedule(self, window=48, lat=LAT, nphys=7):
        rem = {e: list(self.ops[e]) for e in self.ENGS}
        out = {e: [] for e in self.ENGS}
        free = {e: 0.0 for e in self.ENGS}
        total = sum(len(v) for v in rem.values())
        done = 0
        rdy = {}
        blk = {}
        issue = {"sp": 100.0, "pool": 1200.0}
        occ = [None] * nphys

        vbs_all = self.vbs_all
        vptr = [0]
        RESERVE = 2

        def oldest_vb():
            i = vptr[0]
            while i < len(vbs_all) and vbs_all[i].phys is not None:
                i += 1
            vptr[0] = i
            return vbs_all[i] if i < len(vbs_all) else None

        def pick_phys(vb):
            bp = None
            bt = None
            nfree = 0
            for p in range(nphys):
                u = occ[p]
                if u is None:
                    t = 0.0
                elif u.nsched == len(u.ops):
                    t = u.lastfin
                else:
                    continue
                nfree += 1
                if bt is None or t < bt:
                    bt = t
                    bp = p
            if bp is not None and nfree < 1 + RESERVE and vb is not oldest_vb():
                return None, None
            return bp, bt

        while done < total:
            best = None
            for e in self.ENGS:
                lst = rem[e]
                if not lst:
                    continue
                cand = None
                fe = free[e]
                for k in range(min(window, len(lst))):
                    op = lst[k]
                    oid = id(op)
                    ready = rdy.get(oid)
                    if ready is None:
                        b_ = blk.get(oid)
                        if b_ is not None and b_.fin is None:
                            continue
                        ready = 0.0
                        ok = True
                        for p in op.deps:
                            if p.fin is None:
                                blk[oid] = p
                                ok = False
                                break
                            t = p.fin + lat
                            if t > ready:
                                ready = t
                        if ok:
                            for p in op.odeps:
                                if p.fin is None:
                                    blk[oid] = p
                                    ok = False
                                    break
                                if p.fin > ready:
                                    ready = p.fin
                        if not ok:
                            continue
                        rdy[oid] = ready
                    newvb = False
                    for vb in op.vbs:
                        if vb.phys is None:
                            newvb = True
                            bp, bt = pick_phys(vb)
                            if bp is None:
                                ready = None
                            elif bt + lat > ready:
                                ready = bt + lat
                            break
                    if ready is None:
                        continue
                    st = ready if ready > fe else fe
                    if cand is None or st < cand[0] - 1e-9:
                        cand = (st, k, op)
                    if st <= fe:
                        break
                if cand is not None and (best is None or cand[0] < best[0][0]):
                    best = (cand, e)
            (st, k, op), e = best
            rem[e].pop(k)
            out[e].append(op)
            for vb in op.vbs:
                if vb.phys is None:
                    bp, bt = pick_phys(vb)
                    u = occ[bp]
                    if u is not None:
                        last = {}
                        for q in u.ops:
                            kk = q.eng if q.dma is None else ("d", id(q))
                            if kk not in last or q.fin > last[kk].fin:
                                last[kk] = q
                        for q in last.values():
                            if q.eng == "pe" and op.eng == "pe" and q.dma is None:
                                op.odeps.append(q)
                            else:
                                op.deps.append(q)
                    occ[bp] = vb
                    vb.phys = bp
            op.fin = st + op.cost
            for vb in op.vbs:
                vb.nsched += 1
                if op.fin > vb.lastfin:
                    vb.lastfin = op.fin
            if op.dma is not None:
                free[e] = st + issue.get(e, 100.0)
            else:
                free[e] = op.fin
            done += 1
        self.ops = out
        self.est_ns = max(o.fin for e in self.ENGS for o in out[e])
        return self.est_ns


def build_program():
    nc = bass.Bass("TRN2", target_bir_lowering=False)
    S = Sched()

    def din(name, shape):
        return nc.dram_tensor(name, list(shape), F32, kind="ExternalInput").ap()

    xw = din("xw", (NCH, W, D))
    w_in = din("w_in", (D, DIN))
    wba = din("wba", (512, D))
    wbb = din("wbb", (512, D))
    wout = din("wout", (D, D))
    ngd = din("ng", (128, 8))
    gcd = din("gcols", (128, 4))
    ropec = din("ropec", (NCH, 128, W))
    ropes = din("ropes", (NCH, 128, W))
    natab = din("natab", (4, 4, 128, 1920))
    vfld = din("vfl", (NCH, 128, NVF))
    cmat = din("cmat", (3, 128, 128))
    bmask = din("bmask", (128, 512))
    yo = nc.dram_tensor("yo", [NCH, NQ, D], F32, kind="ExternalOutput").ap()

    import contextlib
    es = contextlib.ExitStack()

    def sb(name, cols, dt):
        return es.enter_context(nc.sbuf_tensor(name, [128, cols], dt))

    def psb(name, cols, dt):
        return es.enter_context(nc.psum_tensor(name, [128, cols], dt))

    with es:
        hT = sb("hT", 8 * W, BF16)
        og = sb("og", 2 * 4 * NQ, BF16)
        NSLOT = 6
        Wr = sb("Wr", NSLOT * 1024, BF16)
        AR = sb("AR", 16384, BF16)
        RT = sb("RT", 8192, BF16)
        NAT = sb("NAT", 3840, BF16)
        XA = sb("XA", 4096, F32)
        PT = sb("PT", 4 * 512, BF16)
        sq = sb("sq", NSC * 512, BF16)
        rs = sb("rs", NSC * 512, F32)
        qn = sb("qn", NSC * 512, BF16)
        t1 = sb("t1", NSC * 512, F32)
        t2 = sb("t2", NSC * 512, F32)
        xn = sb("xn", 2 * 1024, BF16)
        st_ = sb("stats", 16, F32)
        ng = sb("ngs", 8, F32)
        gc = sb("gcs", 4, F32)
        cm = sb("cm", 3 * 128, BF16)
        bm = sb("bm", 512, BF16)
        vfl = sb("vfls", NVF, F32)

        psT = psb("psT", 512, F32)
        psA = [psb("psA%d" % i, 512, F32) for i in range(2)]
        psX = [psb("psX%d" % i, 512, F32) for i in range(2)]
        psS = [psb("psS%d" % i, 512, F32) for i in range(2)]
        psO = psb("psO", 512, F32)
        psO0 = psO

        FREE = {"hT": 8 * W, "og": 2 * 4 * NQ, "Wr": NSLOT * 1024, "AR": 16384, "RT": 8192, "XA": 4096, "NAT": 3840,
                "PT": 2048, "sq": NSC * 512, "rs": NSC * 512, "qn": NSC * 512, "t1": NSC * 512, "t2": NSC * 512, "junk": 1024,
                "xn": 2048, "stats": 16, "ngs": 8, "gcs": 4, "cm": 384, "bm": 512, "vfls": NVF,
                "psT": 1024, "psO": 512}
        for i in range(2):
            FREE["psA%d" % i] = 512
            FREE["psX%d" % i] = 512
            FREE["psS%d" % i] = 512
        TN = {}

        def V(t, off, dims, p0=0, pn=128):
            if isinstance(t, VB):
                return LazyAP(t, off, dims, p0, pn)
            Fz = FREE[t.name]
            return bass.AP(t, p0 * Fz + off, [[Fz, pn]] + [[s, c] for s, c in dims])

        def RS(x):
            if isinstance(x, LazyAP):
                t = banks[x.vb.phys]
                if x.bf is not None:
                    vw = t[:, :].bitcast(BF16)
                    if x.bf[0] == "cols":
                        return vw[:, x.bf[1]:x.bf[2]]
                    return vw[:, 0:1024].rearrange("p (c t) -> p c t", c=8)
                return bass.AP(t, x.p0 * 512 + x.off, [[512, x.pn]] + [[s_, c_] for s_, c_ in x.dims])
            return x

        def VT(vb, a, b):
            return LazyAP(vb, 0, [(1, b - a)], 0, 128, bf=("cols", a, b))

        def VT8(vb):
            return LazyAP(vb, 0, [(128, 8), (1, 128)], 0, 128, bf=("c8",))

        def nfree(ap):
            n = 1
            for d_ in ap.shape[1:]:
                n *= d_
            return n

        def _is_fast(ap):
            return (not isinstance(ap, LazyAP)) and ap.dtype == BF16 and ap.ap[-1][0] == 1

        def _strided(ap):
            return (not isinstance(ap, LazyAP)) and ap.ap[-1][0] not in (0, 1)

        def vcost(eng, out, *ins):
            n = nfree(out)
            if eng != "dve":
                return 150.0 + 1.9 * n
            aps = [a_ for a_ in (out,) + ins if hasattr(a_, "shape")]
            if all(_is_fast(a_) for a_ in aps):
                return 120.0 + 0.55 * n
            if any(_strided(a_) for a_ in aps):
                return 150.0 + 1.9 * n
            return 100.0 + 1.15 * n

        def mm(out, lhsT, rhs, start, stop, reads, writes):
            S.emit("pe", lambda e: e.matmul(RS(out), RS(lhsT), RS(rhs), start=start, stop=stop), reads, writes,
                   cost=8.0 + 0.43 * max(nfree(out), 100))

        def tr(out, in_, ident, reads, writes):
            S.emit("pe", lambda e: e.transpose(RS(out), in_, ident), reads, writes, cost=70.0)

        def act(out, in_, func, reads, writes, scale=1.0, bias=0.0, accum=None):
            c = 170.0 + 0.82 * nfree(out)
            if accum is None:
                S.emit("act", lambda e: e.activation(RS(out), RS(in_), func, bias=bias, scale=scale), reads, writes, cost=c)
            else:
                c = 170.0 + 0.82 * nfree(in_)
                S.emit("act", lambda e: e.activation(RS(out), RS(in_), func, bias=bias, scale=scale, accum_out=accum),
                       reads, writes, cost=c)

        def tt(eng, out, in0, in1, op, reads, writes):
            S.emit(eng, lambda e: e.tensor_tensor(RS(out), RS(in0), RS(in1), op), reads, writes, cost=vcost(eng, out, in0, in1))

        def ts(eng, out, in0, s1, s2, op0, op1, reads, writes):
            if s2 is None:
                S.emit(eng, lambda e: e.tensor_scalar(RS(out), RS(in0), s1, None, op0), reads, writes, cost=vcost(eng, out))
            else:
                S.emit(eng, lambda e: e.tensor_scalar(RS(out), RS(in0), s1, s2, op0, op1), reads, writes,
                       cost=vcost(eng, out))

        def stt(eng, out, in0, scalar, in1, op0, op1, reads, writes):
            S.emit(eng, lambda e: e.scalar_tensor_tensor(RS(out), RS(in0), scalar, RS(in1), op0, op1), reads, writes,
                   cost=vcost(eng, out))

        def recip(out, in_, reads, writes):
            S.emit("dve", lambda e: e.reciprocal(out, in_), reads, writes, cost=100.0 + 5.5 * nfree(out))

        def cp(eng, out, in_, reads, writes):
            S.emit(eng, lambda e: e.tensor_copy(RS(out), RS(in_)), reads, writes, cost=vcost(eng, out))

        pool_dmas = []

        def dma(q, out, in_, reads, writes, key):
            op = S.emit(q, lambda e: e.dma_start(out=out, in_=in_), reads, writes, dma=key,
                        cost=2500.0 + 0.02 * 128 * nfree(out) if out.shape[0] == 128 else 4000.0)
            if q == "pool":
                if len(pool_dmas) >= MAX_SWDGE_INFLIGHT:
                    op.deps.append(pool_dmas[-MAX_SWDGE_INFLIGHT])
                pool_dmas.append(op)

        A = S.A
        R = S.R

        dma("pool", V(cm, 0, [(128, 3), (1, 128)]), cmat.rearrange("k p c -> p k c"), [], [A("cm")], "c0")
        dma("pool", V(bm, 0, [(1, 512)]), bmask, [], [A("bm")], "c1")
        dma("sp", V(ng, 0, [(1, 8)]), ngd, [], [A("ng")], "c2")
        dma("sp", V(gc, 0, [(1, 4)]), gcd, [], [A("gc")], "c3")
        ts("dve", V(gc, 0, [(2, 2)]), V(gc, 0, [(2, 2)]), 0.125, None, ALU.mult, None, [A("gc")], [A("gc")])
        IDENT = V(cm, 0, [(1, 128)])
        PERM = V(cm, 128, [(1, 128)])
        BD = V(cm, 256, [(1, 128)])

        wstate = {"n": 0}

        def load_w(src_ap_fn, nchunk=8, ncol=128):
            s = wstate["n"] % NSLOT
            wstate["n"] += 1
            dma("pool", V(Wr, s * 1024, [(ncol, nchunk), (1, ncol)]), src_ap_fn, [], [A("W", s)], "w%d" % s)
            return s

        w_in_v = w_in.rearrange("(c p) e -> p c e", p=128)

        def w_in_cols(col0, ncol=128):
            return w_in_v[:, :, col0:col0 + ncol]

        def hT_tiles(t0, t1_):
            return [A("hT", i) for i in range(t0 // 128, (t1_ - 1) // 128 + 1)]

        acc_rr = {"n": 0}
        aux_rr = {"n": 0}
        sc_rr = {"n": 0}
        st_rr = {"n": 0}
        pt_rr = {"n": 0}

        banks = [psA[0], psA[1], psX[0], psX[1], psS[0], psS[1], psO0, psT]

        def next_bank():
            vb = VB()
            S.vbs_all.append(vb)
            return vb, vb.atom

        next_acc = next_bank
        next_aux = next_bank
        next_st = next_bank

        def proj_fm(slot, t0, N):
            ps, pa = next_acc()
            for c in range(8):
                mm(V(ps, 0, [(1, N)]), V(Wr, slot * 1024 + c * 128, [(1, 128)]),
                   V(hT, c * W + t0, [(1, N)]), c == 0, c == 7,
                   [A("W", slot)] + hT_tiles(t0, t0 + N), [pa])
            return ps, pa

        def rsqrt_chain(ps, pa, N, sl):
            o = sl * 512
            act(V(sq, o, [(1, N)]), V(ps, 0, [(1, N)]), AF.Square, [pa], [A("sq", sl)])
            px, pxa = next_aux()
            mm(V(px, 0, [(1, N)]), BD, V(sq, o, [(1, N)]), True, True, [A("cm"), A("sq", sl)], [pxa])
            act(V(rs, o, [(1, N)]), V(px, 0, [(1, N)]), AF.Ln, [pxa], [A("rs", sl)], scale=1.0, bias=EPS)
            act(V(rs, o, [(1, N)]), V(rs, o, [(1, N)]), AF.Exp, [A("rs", sl)], [A("rs", sl)], scale=-0.5)

        def qk_tile(slot, t0, N, gcol, dst_off, d, sub, n_off, cidx):
            ps, pa = proj_fm(slot, t0, N)
            sl = sc_rr["n"] % NSC
            sc_rr["n"] += 1
            o = sl * 512
            rsqrt_chain(ps, pa, N, sl)
            if d == 0:
                c0 = dst_off + (t0 - n_off)
                stt("dve", V(AR, c0, [(1, N)]), V(ps, 0, [(1, N)]), V(gc, gcol, [(1, 1)]), V(rs, o, [(1, N)]),
                    ALU.mult, ALU.mult, [pa, A("gc"), A("rs", sl)], R("AR", c0, c0 + N))
                return
            stt("dve", V(qn, o, [(1, N)]), V(ps, 0, [(1, N)]), V(gc, gcol, [(1, 1)]), V(rs, o, [(1, N)]),
                ALU.mult, ALU.mult, [pa, A("gc"), A("rs", sl)], [A("qn", sl)])
            px, pxa = next_aux()
            mm(V(px, 0, [(1, N)]), PERM, V(qn, o, [(1, N)]), True, True, [A("cm"), A("qn", sl)], [pxa])
            tt("pool", V(t1, o, [(1, N)]), V(qn, o, [(1, N)]), V(RT, t0, [(1, N)]), ALU.mult,
               [A("qn", sl)] + R("RT", t0, t0 + N), [A("t1", sl)])
            tt("dve", V(t2, o, [(1, N)]), V(px, 0, [(1, N)]), V(RT, W + t0, [(1, N)]), ALU.mult,
               [pxa] + R("RT", W + t0, W + t0 + N), [A("t2", sl)])
            n0 = t0 // d - n_off
            cnt = N // d
            wr = []
            for r in range(d):
                wr += R("AR", dst_off + r * sub + n0, dst_off + r * sub + n0 + cnt)
            tt("pool" if (ROPE_ADD_POOL and (sc_rr["n"] % ROPE_ADD_POOL == 0)) else "dve", V(AR, dst_off + n0, [(sub, d), (1, cnt)]),
               V(t1, o, [(1, d), (d, cnt)]), V(t2, o, [(1, d), (d, cnt)]), ALU.add,
               [A("t1", sl), A("t2", sl)], wr)

        QT, KT, VV, SG = 0, 2048, 6144, 14336

        def v_blocks(slot, nblk, tok_fn, d, vf_col):
            cp("dve", V(AR, VV + 64, [(256, nblk), (64, 2), (1, 64)]),
               V(vfl, vf_col, [(1, nblk), (0, 2), (0, 64)]), [A("vfl")], [A("VFLG")])
            for b0 in range(0, nblk, 4):
                nb = min(4, nblk - b0)
                ps, pa = next_acc()
                for jj in range(nb):
                    tk = tok_fn(b0 + jj)
                    for c in range(8):
                        mm(V(ps, jj * 128, [(1, 128)]), V(hT, c * W + tk, [(d, 128)]),
                           V(Wr, slot * 1024 + c * 128, [(1, 128)]), c == 0, c == 7,
                           [A("W", slot)] + hT_tiles(tk, tk + 127 * d + 1), [pa])
                c0 = VV + b0 * 256
                wr = R("AR", c0, c0 + nb * 256)
                cp("dve", V(AR, c0, [(256, nb), (192, 2), (1, 64)]), V(ps, 0, [(128, nb), (64, 2), (1, 64)]),
                   [pa], wr)

        def gate_tiles(slot):
            for i in range(4):
                t0 = Q0 + 512 * i
                ps, pa = proj_fm(slot, t0, 512)
                sl = sc_rr["n"] % NSC
                sc_rr["n"] += 1
                o = sl * 512
                act(V(t1, o, [(1, 512)]), V(ps, 0, [(1, 512)]), AF.Exp, [pa], [A("t1", sl)], scale=-1.0)
                act(V(t1, o, [(1, 512)]), V(t1, o, [(1, 512)]), AF.Ln, [A("t1", sl)], [A("t1", sl)], bias=1.0)
                act(V(t1, o, [(1, 512)]), V(t1, o, [(1, 512)]), AF.Exp, [A("t1", sl)], [A("t1", sl)], scale=-1.0)
                c0 = SG + 512 * i
                tt("dve", V(AR, c0, [(1, 512)]), V(ps, 0, [(1, 512)]), V(t1, o, [(1, 512)]), ALU.mult,
                   [pa, A("t1", sl)], R("AR", c0, c0 + 512))

        mul_rr = {"n": 0}
        nat_rr = {"n": 0}

        def mul_eng():
            return "dve"

        OUTER = list(range(0, 8)) + list(range(24, 32))
        INNER = list(range(8, 24))

        def phase_a(ch, tiles):
            for ti in tiles:
                xs = ti % 4
                dma("sp", V(XA, xs * 1024, [(1, 1024)]), xw[ch, ti * 128:(ti + 1) * 128, :], [], [A("XA", xs)],
                    "x%d" % xs)
                ss = ti % 2
                act(V(xn, ss * 1024, [(1, 1024)]), V(XA, xs * 1024, [(1, 1024)]), AF.Square, [A("XA", xs)],
                    [A("xn", ss), A("st", ss)], accum=V(st_, ss, [(1, 1)]))
                act(V(st_, 2 + ss, [(1, 1)]), V(st_, ss, [(1, 1)]), AF.Ln, [A("st", ss)], [A("st2", ss)],
                    scale=1.0 / D, bias=EPS)
                act(V(st_, 4 + ss, [(1, 1)]), V(st_, 2 + ss, [(1, 1)]), AF.Exp, [A("st2", ss)], [A("st3", ss)],
                    scale=-0.5)
                ts("dve", V(xn, ss * 1024, [(1, 1024)]), V(XA, xs * 1024, [(1, 1024)]), V(st_, 4 + ss, [(1, 1)]),
                   None, ALU.mult, None, [A("XA", xs), A("st3", ss)], [A("xn", ss)])
                vbT, vbTa = next_bank()
                for c in range(8):
                    tr(VT(vbT, c * 128, (c + 1) * 128), V(xn, ss * 1024 + c * 128, [(1, 128)]), IDENT,
                       [A("xn", ss), A("cm")], [vbTa])
                tt("dve", V(hT, ti * 128, [(W, 8), (1, 128)]), VT8(vbT),
                   V(ng, 0, [(1, 8), (0, 128)]), ALU.mult, [vbTa, A("ng")], [A("hT", ti)])

        for ch in range(NCH):
            if ch == 0:
                dma("sp", V(vfl, 0, [(1, NVF)]), vfld[ch], [], [A("vfl")], "vf")
                phase_a(ch, INNER)
                phase_a(ch, OUTER)
            else:
                phase_a(ch, INNER)

            dma("pool", V(RT, 0, [(1, W)]), ropec[ch], [], R("RT", 0, W), "rc")
            dma("pool", V(RT, W, [(1, W)]), ropes[ch], [], R("RT", W, 2 * W), "rs")
            for jp in range(4):
                cb = jp * 128
                if ch == 0:
                    tv = (1, 0, 0)
                elif ch == 1:
                    tv = (0, 0, 1)
                else:
                    tv = (2, 0, 3)
                s_q = load_w(w_in_cols(QA + cb))
                s_k = load_w(w_in_cols(KA + cb))
                s_v = load_w(w_in_cols(VA + cb))
                s_g = load_w(w_in_cols(GA + cb))
                for i in range(4):
                    qk_tile(s_q, Q0 + 512 * i, 512, 0, QT, 0, 0, Q0, 0)
                for i in range(5):
                    qk_tile(s_k, 768 + 512 * i, 512, 1, KT, 0, 0, 0, 0)
                v_blocks(s_v, 20, lambda b: 768 + 128 * b, 1, VF_OFF["na"])
                gate_tiles(s_g)
                def load_tab(var, h2, dst):
                    xs = nat_rr["n"] % 4
                    nat_rr["n"] += 1
                    dma("sp", V(XA, xs * 1024, [(1, 960)]), natab[jp, var, :, h2 * 960:(h2 + 1) * 960],
                        [], [A("XA", xs)], "x%d" % xs)
                    act(V(NAT, dst * 960, [(1, 960)]), V(XA, xs * 1024, [(1, 960)]), AF.Exp, [A("XA", xs)],
                        [A("NAT", dst)])
                for h2 in range(2):
                    load_tab(tv[1], h2, h2)
                    load_tab(tv[0], h2, 2 + h2)
                for hh in range(2):
                    pq = hh * 64
                    pd = (1 - hh) * 64
                    for rg in range(8):
                        if rg == 7:
                            load_tab(tv[2], hh, 2 + hh)
                        tslot = (2 + hh) if rg in (0, 7) else hh
                        if rg % 2 == 0:
                            psO, psOa = next_bank()
                        oo = 256 * (rg % 2)
                        for pp in range(3):
                            stp, sta = next_st()
                            for slot_i, kb in ((0, 2 * pp + 1), (1, 2 * pp)):
                                j = 2 * rg + kb
                                kc = KT + 768 + 128 * j
                                mm(V(stp, slot_i * 256, [(1, 256)]), V(AR, kc, [(1, 128)], pq, 64),
                                   V(AR, QT + 256 * rg, [(1, 256)], pq, 64), True, True,
                                   R("AR", kc, kc + 128) + R("AR", QT + 256 * rg, QT + 256 * rg + 256), [sta])
                            pts = pt_rr["n"] % 4
                            pt_rr["n"] += 1
                            act(V(PT, pts * 512, [(1, 512)]), V(stp, 0, [(1, 512)]), AF.Exp, [sta], [A("PT", pts)])
                            tc0 = tslot * 960 + (9 - 4 * pp) * 64
                            tt(mul_eng(), V(PT, pts * 512, [(256, 2), (1, 256)]),
                               V(PT, pts * 512, [(256, 2), (1, 256)]), V(NAT, tc0, [(128, 2), (1, 256)]), ALU.mult,
                               [A("PT", pts), A("NAT", tslot)], [A("PT", pts)])
                            for slot_i, kb in ((0, 2 * pp + 1), (1, 2 * pp)):
                                j = 2 * rg + kb
                                vc = VV + j * 256 + hh * 128
                                first = (pp == 0 and slot_i == 0)
                                last = (pp == 2 and slot_i == 1)
                                mm(V(psO, oo, [(1, 256)]), V(AR, vc, [(1, 128)]),
                                   V(PT, pts * 512 + slot_i * 256, [(1, 256)]), first, last,
                                   R("AR", vc, vc + 128) + [A("PT", pts), A("VFLG")], [psOa])
                        if rg % 2 == 0:
                            continue
                        sl = sc_rr["n"] % NSC
                        sc_rr["n"] += 1
                        o = sl * 512
                        g0 = 256 * (rg - 1)
                        act(V(rs, o, [(1, 512)], pq, 64), V(psO, 0, [(1, 512)], pd, 64), AF.Ln, [psOa],
                            [A("rs", sl)])
                        act(V(rs, o, [(1, 512)], pq, 64), V(rs, o, [(1, 512)], pq, 64), AF.Exp, [A("rs", sl)],
                            [A("rs", sl)], scale=-1.0)
                        tt("dve", V(t1, o, [(1, 512)], pq, 64), V(psO, 0, [(1, 512)], pq, 64),
                           V(rs, o, [(1, 512)], pq, 64), ALU.mult, [psOa, A("rs", sl)], [A("t1", sl)])
                        oc = (0 * 4 + jp) * NQ + g0
                        tt(FIN_ENG, V(og, oc, [(1, 512)], pq, 64), V(t1, o, [(1, 512)], pq, 64),
                           V(AR, SG + g0, [(1, 512)], pq, 64), ALU.mult,
                           [A("t1", sl)] + R("AR", SG + g0, SG + g0 + 512),
                           [A("og", 0, jp, rg // 2)])

                s_g = load_w(w_in_cols(GB + cb))
                gate_tiles(s_g)
                for gi, (d, qc, kc_, vc_) in enumerate(DIL):
                    sub = W // d
                    qsub = NQ // d
                    nj = 16 // d + 1
                    s_q = load_w(w_in_cols(qc + cb))
                    s_k = load_w(w_in_cols(kc_ + cb))
                    s_v = load_w(w_in_cols(vc_ + cb))
                    for i in range(4):
                        qk_tile(s_q, Q0 + 512 * i, 512, 2, QT, d, qsub, Q0 // d, 0)
                    if d == 1:
                        ktiles = [(960 + 512 * i, 512) for i in range(4)] + [(3008, 128)]
                    elif d == 4:
                        ktiles = [(768 + 512 * i, 512) for i in range(5)]
                    else:
                        ktiles = [(512 * i, 512) for i in range(8)]
                    for (t0, N) in ktiles:
                        qk_tile(s_k, t0, N, 3, KT, d, sub, 0, 0)
                    kbase = Q0 // d - 64

                    def tokfn(b, d=d, nj=nj, kbase=kbase):
                        r, j = divmod(b, nj)
                        return (kbase + 128 * j) * d + r
                    v_blocks(s_v, d * nj, tokfn, d, VF_OFF[d])
                    nb_r = 16 // d
                    for hh in range(2):
                        pq = hh * 64
                        for qp in range(8):
                            stp, sta = next_st()
                            psO, psOa = next_bank()
                            info = []
                            for qi in range(2):
                                qb = 2 * qp + qi
                                r, b = divmod(qb, nb_r)
                                info.append((r, b))
                                qcol = QT + r * qsub + 128 * b
                                for half in range(2):
                                    j = b + half
                                    kcol = KT + r * sub + kbase + 128 * j
                                    mm(V(stp, qi * 256 + half * 128, [(1, 128)]), V(AR, kcol, [(1, 128)], pq, 64),
                                       V(AR, qcol, [(1, 128)], pq, 64), True, True,
                                       R("AR", kcol, kcol + 128) + R("AR", qcol, qcol + 128), [sta])
                            pts = pt_rr["n"] % 4
                            pt_rr["n"] += 1
                            act(V(PT, pts * 512, [(1, 512)]), V(stp, 0, [(1, 512)]), AF.Exp, [sta], [A("PT", pts)])
                            tt(mul_eng(), V(PT, pts * 512, [(1, 512)]), V(PT, pts * 512, [(1, 512)]),
                               V(bm, 0, [(1, 512)]), ALU.mult, [A("PT", pts), A("bm")], [A("PT", pts)])
                            for qi in range(2):
                                r, b = info[qi]
                                for half in range(2):
                                    kbi = r * nj + b + half
                                    vc = VV + kbi * 256 + hh * 128
                                    mm(V(psO, qi * 128, [(1, 128)]), V(AR, vc, [(1, 128)]),
                                       V(PT, pts * 512 + qi * 256 + half * 128, [(1, 128)]), half == 0, half == 1,
                                       R("AR", vc, vc + 128) + [A("PT", pts), A("VFLG")], [psOa])
                            (r0, b0), (r1, b1) = info
                            off0 = r0 + 128 * b0 * d
                            off1 = r1 + 128 * b1 * d
                            accv = V(XA, hh * 2048 + off0, [(off1 - off0, 2), (d, 128)])
                            acca = [A("XA", 2 * hh), A("XA", 2 * hh + 1)]
                            if gi == 0:
                                cp("dve", accv, V(psO, 0, [(128, 2), (1, 128)]), [psOa], acca)
                            else:
                                tt("dve", accv, V(psO, 0, [(128, 2), (1, 128)]), accv, ALU.add,
                                   [psOa] + acca, acca)
                for hh in range(2):
                    pq = hh * 64
                    pd = (1 - hh) * 64
                    for i in range(4):
                        sl = sc_rr["n"] % NSC
                        sc_rr["n"] += 1
                        o = sl * 512
                        acca = [A("XA", 2 * hh), A("XA", 2 * hh + 1)]
                        act(V(rs, o, [(1, 512)], pq, 64), V(XA, hh * 2048 + 512 * i, [(1, 512)], pd, 64), AF.Ln,
                            acca, [A("rs", sl)])
                        act(V(rs, o, [(1, 512)], pq, 64), V(rs, o, [(1, 512)], pq, 64), AF.Exp, [A("rs", sl)],
                            [A("rs", sl)], scale=-1.0)
                        tt("dve", V(t1, o, [(1, 512)], pq, 64), V(XA, hh * 2048 + 512 * i, [(1, 512)], pq, 64),
                           V(rs, o, [(1, 512)], pq, 64), ALU.mult, acca + [A("rs", sl)], [A("t1", sl)])
                        oc = (1 * 4 + jp) * NQ + 512 * i
                        tt(FIN_ENG, V(og, oc, [(1, 512)], pq, 64), V(t1, o, [(1, 512)], pq, 64),
                           V(AR, SG + 512 * i, [(1, 512)], pq, 64), ALU.mult,
                           [A("t1", sl)] + R("AR", SG + 512 * i, SG + 512 * i + 512), [A("og", 1, jp, i)])

            if ch + 1 < NCH:
                dma("sp", V(vfl, 0, [(1, NVF)]), vfld[ch + 1], [], [A("vfl")], "vf")
                phase_a(ch + 1, OUTER)
            dma("pool", V(RT, 0, [(1024, 8), (1, 1024)]), wout.rearrange("(c p) e -> p c e", p=128), [],
                R("RT", 0, 8192), "wo")
            for ec in range(8):
                s_ma = load_w(w_in_cols(MA + ec * 128))
                s_mb = load_w(w_in_cols(MB + ec * 128))
                s_ba = load_w(wba.rearrange("(c p) e -> p c e", p=128)[:, :, ec * 128:(ec + 1) * 128], nchunk=4)
                s_bb = load_w(wbb.rearrange("(c p) e -> p c e", p=128)[:, :, ec * 128:(ec + 1) * 128], nchunk=4)
                for tq in range(4):
                    t0 = Q0 + 512 * tq
                    pma, pmaa = proj_fm(s_ma, t0, 512)
                    pmb, pmba = proj_fm(s_mb, t0, 512)
                    pba, pbaa = next_aux()
                    pbb, pbba = next_aux()
                    for ab, (pb_, pba_, sslot) in enumerate(((pba, pbaa, s_ba), (pbb, pbba, s_bb))):
                        for jp in range(4):
                            mm(V(pb_, 0, [(1, 512)]), V(Wr, sslot * 1024 + jp * 128, [(1, 128)]),
                               V(og, (ab * 4 + jp) * NQ + 512 * tq, [(1, 512)]), jp == 0, jp == 3,
                               [A("W", sslot), A("og", ab, jp, tq)], [pba_])
                    sl = sc_rr["n"] % NSC
                    sc_rr["n"] += 1
                    o = sl * 512
                    for (pm, pma_, pb_, pba_, tb, tbn) in ((pma, pmaa, pba, pbaa, t1, "t1"),
                                                          (pmb, pmba, pbb, pbba, t2, "t2")):
                        act(V(tb, o, [(1, 512)]), V(pm, 0, [(1, 512)]), AF.Exp, [pma_], [A(tbn, sl)], scale=-1.0)
                        act(V(tb, o, [(1, 512)]), V(tb, o, [(1, 512)]), AF.Ln, [A(tbn, sl)], [A(tbn, sl)], bias=1.0)
                        act(V(tb, o, [(1, 512)]), V(tb, o, [(1, 512)]), AF.Exp, [A(tbn, sl)], [A(tbn, sl)],
                            scale=-1.0)
                        tt("dve", V(tb, o, [(1, 512)]), V(pb_, 0, [(1, 512)]), V(tb, o, [(1, 512)]), ALU.mult,
                           [pba_, A(tbn, sl)], [A(tbn, sl)])
                    mc = ec * NQ + 512 * tq
                    tt("pool" if MERGE_ADD_POOL else "dve", V(AR, mc, [(1, 512)]), V(t1, o, [(1, 512)]), V(t2, o, [(1, 512)]), ALU.add,
                       [A("t1", sl), A("t2", sl)], R("AR", mc, mc + 512) + [A("VFLG")])
            for ti in range(16):
                xs = ti % 4
                tok = Q0 + 128 * ti
                dma("sp", V(XA, xs * 1024, [(1, 1024)]), xw[ch, tok:tok + 128, :], [], [A("XA", xs)], "x%d" % xs)
                for half in range(2):
                    stp, sta = next_st()
                    for ec in range(8):
                        mc = ec * NQ + 128 * ti
                        mm(V(stp, 0, [(1, 512)]), V(AR, mc, [(1, 128)]),
                           V(RT, ec * 1024 + half * 512, [(1, 512)]), ec == 0, ec == 7,
                           R("AR", mc, mc + 128) + R("RT", ec * 1024 + half * 512, ec * 1024 + half * 512 + 512)
                           + [A("VFLG")], [sta])
                    tt("dve", V(XA, xs * 1024 + half * 512, [(1, 512)]), V(stp, 0, [(1, 512)]),
                       V(XA, xs * 1024 + half * 512, [(1, 512)]), ALU.add, [sta, A("XA", xs)], [A("XA", xs)])
                dma("sp", yo[ch, 128 * ti:128 * (ti + 1), :], V(XA, xs * 1024, [(1, 1024)]), [A("XA", xs)], [],
                    "x%d" % xs)

        if DO_SCHED:
            S.schedule(window=WINDOW, nphys=NPHYS)
        else:
            S.est_ns = 0
        print('[sched] est_us=%.1f ops=%s' % (S.est_ns / 1e3, {e: len(S.ops[e]) for e in S.ENGS}), flush=True)
        pos = {}
        for e in S.ENGS:
            for i_, op in enumerate(S.ops[e]):
                pos[id(op)] = i_
        waits_of = {}
        for e in S.ENGS:
            seen = {}
            for op in S.ops[e]:
                grp = {}
                for p in op.deps:
                    kk = p.eng if p.dma is None else ("d", p.dma)
                    q = grp.get(kk)
                    if q is None or pos[id(p)] > pos[id(q)]:
                        grp[kk] = p
                wl = []
                for kk, p in grp.items():
                    if seen.get(kk, -1) >= pos[id(p)]:
                        continue
                    seen[kk] = pos[id(p)]
                    p.needed = True
                    wl.append(p)
                waits_of[id(op)] = wl
        sems = {}
        for e in S.ENGS:
            sems[e] = es.enter_context(nc.semaphore("sem_" + e))
        dsems = {}
        for k in S.dma_keys:
            dsems[k] = es.enter_context(nc.semaphore("dsem_" + k))
        dcount = {k: 0 for k in S.dma_keys}
        nsig = {}
        for e in S.ENGS:
            cnt = 0
            for op in S.ops[e]:
                if op.dma is not None:
                    continue
                if op.needed:
                    cnt += 1
                    op.sig = (sems[e], cnt)
            nsig[e] = cnt
        for e in S.ENGS:
            for op in S.ops[e]:
                if op.dma is not None:
                    dcount[op.dma] += 16
                    op.sig = (dsems[op.dma], dcount[op.dma])
        print('[sync] signalling ops per engine:', nsig, flush=True)

        def run_engine(ename, eh):
            for op in S.ops[ename]:
                for p in waits_of[id(op)]:
                    eh.wait_ge(p.sig[0], p.sig[1])
                ins = op.fn(eh)
                if op.dma is not None:
                    ins.then_inc(op.sig[0], 16)
                elif op.needed:
                    ins.then_inc(op.sig[0], 1)
            if ename == "sp":
                for k, v in dcount.items():
                    if v > 0:
                        eh.wait_ge(dsems[k], v)

        with nc.Block() as block:
            @block.tensor
            def _(e):
                run_engine("pe", e)

            @block.scalar
            def _(e):
                run_engine("act", e)

            @block.vector
            def _(e):
                run_engine("dve", e)

            @block.gpsimd
            def _(e):
                run_engine("pool", e)

            @block.sync
            def _(e):
                run_engine("sp", e)
    return nc


def _na_tables(rel_bias):
    rb = np.asarray(rel_bias, np.float32)
    p = np.arange(128)
    half = p // 64
    ck = p % 64
    s = np.arange(15)
    cq = np.arange(64)
    drow = 7 - s[None, :] + half[:, None]
    cstart = np.clip(cq - 8, 0, 48)
    colok = (ck[:, None] >= cstart[None, :]) & (ck[:, None] < cstart[None, :] + 16)
    dcol = np.clip(ck[:, None] - cq[None, :], -15, 15) + 15
    out = np.full((4, 4, 128, 2, 15, 64), -30000.0, np.float32)
    for var in range(2):
        if var == 0:
            rowok = (drow >= -4) & (drow <= 3)
        else:
            rowok = (drow >= -7) & (drow <= 7)
        ok = rowok[:, :, None] & colok[:, None, :]
        ridx = np.clip(drow + 7, 0, 14)
        for h in range(8):
            vals = rb[h][ridx[:, :, None], dcol[:, None, :]]
            out[h // 2, var, :, h % 2] = np.where(ok, vals, np.float32(-30000.0))
    return out.reshape(4, 4, 128, 1920)


def _rope_tables(pos):
    inv = (10000.0 ** (-np.arange(0, 64, 2, dtype=np.float32) / np.float32(64))).astype(np.float32)
    ang = (pos.astype(np.float32)[None, :] * inv[:, None]).astype(np.float32)
    c = np.cos(ang.astype(np.float64)).astype(np.float32)
    s = np.sin(ang.astype(np.float64)).astype(np.float32)
    C = np.tile(c, (4, 1))
    Sg = np.concatenate([-s, s, -s, s], axis=0)
    return C, Sg


def _valid_flags(valid):
    out = np.zeros((128, NVF), np.float32)
    i = np.arange(128)
    for j in range(20):
        out[:, VF_OFF["na"] + j] = valid[768 + 128 * j + i]
    for d in (1, 4, 16):
        nj = 16 // d + 1
        kbase = Q0 // d - 64
        for r in range(d):
            for j in range(nj):
                out[:, VF_OFF[d] + r * nj + j] = valid[(kbase + 128 * j + i) * d + r]
    return out


_CACHE = {}


def kernel(x_prompt, x_sample, norm_gain, w_in, qn_a, kn_a, rel_bias_a, qn_b, kn_b, w_branch_a, w_branch_b, w_out):
    x_prompt = np.asarray(x_prompt, np.float32)
    x_sample = np.asarray(x_sample, np.float32)
    if "nc" not in _CACHE:
        _CACHE["nc"] = build_program()
    nc = _CACHE["nc"]

    w_in2 = np.ascontiguousarray(np.asarray(w_in, np.float32)[0])
    wba = np.ascontiguousarray(np.asarray(w_branch_a, np.float32)[0])
    wbb = np.ascontiguousarray(np.asarray(w_branch_b, np.float32)[0])
    wo = np.ascontiguousarray(np.asarray(w_out, np.float32)[0])
    ng = np.ascontiguousarray(np.asarray(norm_gain, np.float32)[0].reshape(8, 128).T)
    gcols = np.stack([np.tile(np.asarray(a, np.float32)[0], 2) for a in (qn_a, qn_b, kn_a, kn_b)], axis=1)
    gcols = np.ascontiguousarray(gcols[:, [0, 2, 1, 3]])
    nat = _na_tables(np.asarray(rel_bias_a, np.float32)[0])
    ident = np.eye(128, dtype=np.float32)
    perm = np.zeros((128, 128), np.float32)
    for m in range(128):
        perm[m, m + 32 if (m % 64) < 32 else m - 32] = 1.0
    bd = np.zeros((128, 128), np.float32)
    bd[:64, :64] = 1.0 / 64
    bd[64:, 64:] = 1.0 / 64
    cmat = np.stack([ident, perm, bd])
    kk = np.arange(128)[:, None]
    qq = np.arange(128)[None, :]
    m1 = np.concatenate([(qq <= kk), (qq >= kk)], axis=1).astype(np.float32)
    bmask = np.ascontiguousarray(np.concatenate([m1, m1], axis=1))

    in_maps = []
    for c in range(NCORES):
        xw = np.zeros((NCH, W, D), np.float32)
        vfl = np.zeros((NCH, 128, NVF), np.float32)
        rc = np.zeros((NCH, 128, W), np.float32)
        rsn = np.zeros((NCH, 128, W), np.float32)
        for ch, (src, L, q0) in enumerate(((x_prompt[c], 4096, 0), (x_prompt[c], 4096, 2048),
                                           (x_sample[0], 16384, 2048 * c))):
            lo = q0 - Q0
            pos = np.arange(lo, lo + W)
            valid = ((pos >= 0) & (pos < L))
            a = max(lo, 0)
            b = min(lo + W, L)
            xw[ch, a - lo:b - lo] = src[a:b]
            vfl[ch] = _valid_flags(valid.astype(np.float32))
            C, Sg = _rope_tables(np.clip(pos, 0, L - 1))
            rc[ch] = C
            rsn[ch] = Sg
        natc = nat.copy()
        natc[:, 2] = nat[:, 1] if c == 0 else nat[:, 0]
        natc[:, 3] = nat[:, 1] if c == NCORES - 1 else nat[:, 0]
        in_maps.append({"xw": xw, "w_in": w_in2, "wba": wba, "wbb": wbb, "wout": wo, "ng": ng, "gcols": gcols,
                        "ropec": rc, "ropes": rsn, "natab": natc, "vfl": vfl, "cmat": cmat, "bmask": bmask})

    res = run_bass_kernel_spmd(nc, in_maps, core_ids=list(range(NCORES)))
    y_prompt = np.empty((8, 4096, D), np.float32)
    y_sample = np.empty((1, 16384, D), np.float32)
    for c in range(NCORES):
        yo = res.results[c]["yo"]
        y_prompt[c, 0:2048] = yo[0]
        y_prompt[c, 2048:4096] = yo[1]
        y_sample[0, 2048 * c:2048 * (c + 1)] = yo[2]
    return (y_prompt, y_sample)
```

```python
import numpy as np
import concourse.bass as bass
import concourse.mybir as mybir
from concourse.bass_utils import run_bass_kernel_spmd

F32 = mybir.dt.float32
BF16 = mybir.dt.bfloat16
AF = mybir.ActivationFunctionType
ALU = mybir.AluOpType

NCORES = 8
NPHYS = 8
MAX_SWDGE_INFLIGHT = 3
LAT = 300.0
FIN_ENG = "dve"
ROPE_ADD_POOL = 0
MERGE_ADD_POOL = 0
NSC = 2
WINDOW = 256
DO_SCHED = True
D = 1024
DIN = 9216
W = 4096
Q0 = 1024
NQ = 2048
NCH = 3
EPS = 1e-6
QA, KA, VA, GA = 0, 512, 1024, 1536
DIL = ((1, 2048, 2560, 3072), (4, 3584, 4096, 4608), (16, 5120, 5632, 6144))
GB, MA, MB = 6656, 7168, 8192
NVF = 20 + 17 + 20 + 32
VF_OFF = {"na": 0, 1: 20, 4: 37, 16: 57}


class Atom:
    __slots__ = ("lw", "rd", "vb")

    def __init__(self):
        self.lw = None
        self.rd = {}
        self.vb = None


class VB:
    __slots__ = ("atom", "ops", "phys", "nsched", "lastfin")

    def __init__(self):
        self.atom = Atom()
        self.atom.vb = self
        self.ops = []
        self.phys = None
        self.nsched = 0
        self.lastfin = 0.0


class LazyAP:
    __slots__ = ("vb", "off", "dims", "p0", "pn", "shape", "bf")

    def __init__(self, vb, off, dims, p0, pn, bf=None):
        self.vb = vb
        self.off = off
        self.dims = dims
        self.p0 = p0
        self.pn = pn
        self.bf = bf
        self.shape = (pn,) + tuple(c for _, c in dims)


class Op:
    __slots__ = ("eng", "fn", "deps", "odeps", "dma", "needed", "sig", "idx", "cost", "fin", "gi", "vbs")

    def __init__(self, eng, fn, dma):
        self.eng = eng
        self.fn = fn
        self.dma = dma
        self.deps = []
        self.odeps = []
        self.needed = False
        self.sig = None
        self.cost = 300.0
        self.fin = None
        self.vbs = []


class Sched:
    ENGS = ("pe", "act", "dve", "pool", "sp")

    def __init__(self):
        self.ops = {e: [] for e in self.ENGS}
        self.atoms = {}
        self.dma_keys = {}
        self.gcount = 0
        self.vbs_all = []

    def A(self, *key):
        a = self.atoms.get(key)
        if a is None:
            a = self.atoms[key] = Atom()
        return a

    def R(self, name, a, b, g=128):
        return [self.A(name, i) for i in range(a // g, (b - 1) // g + 1)]

    def emit(self, eng, fn, reads=(), writes=(), dma=None, cost=300.0):
        op = Op(eng, fn, dma)
        op.idx = len(self.ops[eng])
        op.cost = cost
        op.gi = self.gcount
        self.gcount += 1
        deps = {}
        odeps = {}

        def add(p):
            if p is None:
                return
            if p.eng == "pe" and eng == "pe" and p.dma is None:
                odeps[id(p)] = p
                return
            deps[id(p)] = p

        for b in reads:
            add(b.lw)
        for b in writes:
            add(b.lw)
            for p in b.rd.values():
                add(p)
        op.deps = list(deps.values())
        op.odeps = list(odeps.values())
        for b in list(reads) + list(writes):
            if b.vb is not None and b.vb not in op.vbs:
                op.vbs.append(b.vb)
                b.vb.ops.append(op)
        rk = ("e", id(op))
        for b in reads:
            b.rd[rk] = op
        for b in writes:
            b.lw = op
            b.rd = {}
        self.ops[eng].append(op)
        if dma is not None:
            self.dma_keys.setdefault(dma, 0)
        return op

    def schedule(self, window=48, lat=LAT, nphys=7):
        rem = {e: list(self.ops[e]) for e in self.ENGS}
        out = {e: [] for e in self.ENGS}
        free = {e: 0.0 for e in self.ENGS}
        total = sum(len(v) for v in rem.values())
        done = 0
        rdy = {}
        blk = {}
        issue = {"sp": 100.0, "pool": 1200.0}
        occ = [None] * nphys

        vbs_all = self.vbs_all
        vptr = [0]
        RESERVE = 2

        def oldest_vb():
            i = vptr[0]
            while i < len(vbs_all) and vbs_all[i].phys is not None:
                i += 1
            vptr[0] = i
            return vbs_all[i] if i < len(vbs_all) else None

        def pick_phys(vb):
            bp = None
            bt = None
            nfree = 0
            for p in range(nphys):
                u = occ[p]
                if u is None:
                    t = 0.0
                elif u.nsched == len(u.ops):
                    t = u.lastfin
                else:
                    continue
                nfree += 1
                if bt is None or t < bt:
                    bt = t
                    bp = p
            if bp is not None and nfree < 1 + RESERVE and vb is not oldest_vb():
                return None, None
            return bp, bt

        while done < total:
            best = None
            for e in self.ENGS:
                lst = rem[e]
                if not lst:
                    continue
                cand = None
                fe = free[e]
                for k in range(min(window, len(lst))):
                    op = lst[k]
                    oid = id(op)
                    ready = rdy.get(oid)
                    if ready is None:
                        b_ = blk.get(oid)
                        if b_ is not None and b_.fin is None:
                            continue
                        ready = 0.0
                        ok = True
                        for p in op.deps:
                            if p.fin is None:
                                blk[oid] = p
                                ok = False
                                break
                            t = p.fin + lat
                            if t > ready:
                                ready = t
                        if ok:
                            for p in op.odeps:
                                if p.fin is None:
                                    blk[oid] = p
                                    ok = False
                                    break
                                if p.fin > ready:
                                    ready = p.fin
                        if not ok:
                            continue
                        rdy[oid] = ready
                    newvb = False
                    for vb in op.vbs:
                        if vb.phys is None:
                            newvb = True
                            bp, bt = pick_phys(vb)
                            if bp is None:
                                ready = None
                            elif bt + lat > ready:
                                ready = bt + lat
                            break
                    if ready is None:
                        continue
                    st = ready if ready > fe else fe
                    if cand is None or st < cand[0] - 1e-9:
                        cand = (st, k, op)
                    if st <= fe:
                        break
                if cand is not None and (best is None or cand[0] < best[0][0]):
                    best = (cand, e)
            (st, k, op), e = best
            rem[e].pop(k)
            out[e].append(op)
            for vb in op.vbs:
                if vb.phys is None:
                    bp, bt = pick_phys(vb)
                    u = occ[bp]
                    if u is not None:
                        last = {}
                        for q in u.ops:
                            kk = q.eng if q.dma is None else ("d", id(q))
                            if kk not in last or q.fin > last[kk].fin:
                                last[kk] = q
                        for q in last.values():
                            if q.eng == "pe" and op.eng == "pe" and q.dma is None:
                                op.odeps.append(q)
                            else:
                                op.deps.append(q)
                    occ[bp] = vb
                    vb.phys = bp
            op.fin = st + op.cost
            for vb in op.vbs:
                vb.nsched += 1
                if op.fin > vb.lastfin:
                    vb.lastfin = op.fin
            if op.dma is not None:
                free[e] = st + issue.get(e, 100.0)
            else:
                free[e] = op.fin
            done += 1
        self.ops = out
        self.est_ns = max(o.fin for e in self.ENGS for o in out[e])
        return self.est_ns


def build_program():
    nc = bass.Bass("TRN2", target_bir_lowering=False)
    S = Sched()

    def din(name, shape):
        return nc.dram_tensor(name, list(shape), F32, kind="ExternalInput").ap()

    xw = din("xw", (NCH, W, D))
    w_in = din("w_in", (D, DIN))
    wba = din("wba", (512, D))
    wbb = din("wbb", (512, D))
    wout = din("wout", (D, D))
    ngd = din("ng", (128, 8))
    gcd = din("gcols", (128, 4))
    ropec = din("ropec", (NCH, 128, W))
    ropes = din("ropes", (NCH, 128, W))
    natab = din("natab", (4, 4, 128, 1920))
    vfld = din("vfl", (NCH, 128, NVF))
    cmat = din("cmat", (3, 128, 128))
    bmask = din("bmask", (128, 512))
    yo = nc.dram_tensor("yo", [NCH, NQ, D], F32, kind="ExternalOutput").ap()

    import contextlib
    es = contextlib.ExitStack()

    def sb(name, cols, dt):
        return es.enter_context(nc.sbuf_tensor(name, [128, cols], dt))

    def psb(name, cols, dt):
        return es.enter_context(nc.psum_tensor(name, [128, cols], dt))

    with es:
        hT = sb("hT", 8 * W, BF16)
        og = sb("og", 2 * 4 * NQ, BF16)
        NSLOT = 6
        Wr = sb("Wr", NSLOT * 1024, BF16)
        AR = sb("AR", 16384, BF16)
        RT = sb("RT", 8192, BF16)
        NAT = sb("NAT", 3840, BF16)
        XA = sb("XA", 4096, F32)
        PT = sb("PT", 4 * 512, BF16)
        sq = sb("sq", NSC * 512, BF16)
        rs = sb("rs", NSC * 512, F32)
        qn = sb("qn", NSC * 512, BF16)
        t1 = sb("t1", NSC * 512, F32)
        t2 = sb("t2", NSC * 512, F32)
        xn = sb("xn", 2 * 1024, BF16)
        st_ = sb("stats", 16, F32)
        ng = sb("ngs", 8, F32)
        gc = sb("gcs", 4, F32)
        cm = sb("cm", 3 * 128, BF16)
        bm = sb("bm", 512, BF16)
        vfl = sb("vfls", NVF, F32)

        psT = psb("psT", 512, F32)
        psA = [psb("psA%d" % i, 512, F32) for i in range(2)]
        psX = [psb("psX%d" % i, 512, F32) for i in range(2)]
        psS = [psb("psS%d" % i, 512, F32) for i in range(2)]
        psO = psb("psO", 512, F32)
        psO0 = psO

        FREE = {"hT": 8 * W, "og": 2 * 4 * NQ, "Wr": NSLOT * 1024, "AR": 16384, "RT": 8192, "XA": 4096, "NAT": 3840,
                "PT": 2048, "sq": NSC * 512, "rs": NSC * 512, "qn": NSC * 512, "t1": NSC * 512, "t2": NSC * 512, "junk": 1024,
                "xn": 2048, "stats": 16, "ngs": 8, "gcs": 4, "cm": 384, "bm": 512, "vfls": NVF,
                "psT": 1024, "psO": 512}
        for i in range(2):
            FREE["psA%d" % i] = 512
            FREE["psX%d" % i] = 512
            FREE["psS%d" % i] = 512
        TN = {}

        def V(t, off, dims, p0=0, pn=128):
            if isinstance(t, VB):
                return LazyAP(t, off, dims, p0, pn)
            Fz = FREE[t.name]
            return bass.AP(t, p0 * Fz + off, [[Fz, pn]] + [[s, c] for s, c in dims])

        def RS(x):
            if isinstance(x, LazyAP):
                t = banks[x.vb.phys]
                if x.bf is not None:
                    vw = t[:, :].bitcast(BF16)
                    if x.bf[0] == "cols":
                        return vw[:, x.bf[1]:x.bf[2]]
                    return vw[:, 0:1024].rearrange("p (c t) -> p c t", c=8)
                return bass.AP(t, x.p0 * 512 + x.off, [[512, x.pn]] + [[s_, c_] for s_, c_ in x.dims])
            return x

        def VT(vb, a, b):
            return LazyAP(vb, 0, [(1, b - a)], 0, 128, bf=("cols", a, b))

        def VT8(vb):
            return LazyAP(vb, 0, [(128, 8), (1, 128)], 0, 128, bf=("c8",))

        def nfree(ap):
            n = 1
            for d_ in ap.shape[1:]:
                n *= d_
            return n

        def _is_fast(ap):
            return (not isinstance(ap, LazyAP)) and ap.dtype == BF16 and ap.ap[-1][0] == 1

        def _strided(ap):
            return (not isinstance(ap, LazyAP)) and ap.ap[-1][0] not in (0, 1)

        def vcost(eng, out, *ins):
            n = nfree(out)
            if eng != "dve":
                return 150.0 + 1.9 * n
            aps = [a_ for a_ in (out,) + ins if hasattr(a_, "shape")]
            if all(_is_fast(a_) for a_ in aps):
                return 120.0 + 0.55 * n
            if any(_strided(a_) for a_ in aps):
                return 150.0 + 1.9 * n
            return 100.0 + 1.15 * n

        def mm(out, lhsT, rhs, start, stop, reads, writes):
            S.emit("pe", lambda e: e.matmul(RS(out), RS(lhsT), RS(rhs), start=start, stop=stop), reads, writes,
                   cost=8.0 + 0.43 * max(nfree(out), 100))

        def tr(out, in_, ident, reads, writes):
            S.emit("pe", lambda e: e.transpose(RS(out), in_, ident), reads, writes, cost=70.0)

        def act(out, in_, func, reads, writes, scale=1.0, bias=0.0, accum=None):
            c = 170.0 + 0.82 * nfree(out)
            if accum is None:
                S.emit("act", lambda e: e.activation(RS(out), RS(in_), func, bias=bias, scale=scale), reads, writes, cost=c)
            else:
                c = 170.0 + 0.82 * nfree(in_)
                S.emit("act", lambda e: e.activation(RS(out), RS(in_), func, bias=bias, scale=scale, accum_out=accum),
                       reads, writes, cost=c)

        def tt(eng, out, in0, in1, op, reads, writes):
            S.emit(eng, lambda e: e.tensor_tensor(RS(out), RS(in0), RS(in1), op), reads, writes, cost=vcost(eng, out, in0, in1))

        def ts(eng, out, in0, s1, s2, op0, op1, reads, writes):
            if s2 is None:
                S.emit(eng, lambda e: e.tensor_scalar(RS(out), RS(in0), s1, None, op0), reads, writes, cost=vcost(eng, out))
            else:
                S.emit(eng, lambda e: e.tensor_scalar(RS(out), RS(in0), s1, s2, op0, op1), reads, writes,
                       cost=vcost(eng, out))

        def stt(eng, out, in0, scalar, in1, op0, op1, reads, writes):
            S.emit(eng, lambda e: e.scalar_tensor_tensor(RS(out), RS(in0), scalar, RS(in1), op0, op1), reads, writes,
                   cost=vcost(eng, out))

        def recip(out, in_, reads, writes):
            S.emit("dve", lambda e: e.reciprocal(out, in_), reads, writes, cost=100.0 + 5.5 * nfree(out))

        def cp(eng, out, in_, reads, writes):
            S.emit(eng, lambda e: e.tensor_copy(RS(out), RS(in_)), reads, writes, cost=vcost(eng, out))

        pool_dmas = []

        def dma(q, out, in_, reads, writes, key):
            op = S.emit(q, lambda e: e.dma_start(out=out, in_=in_), reads, writes, dma=key,
                        cost=2500.0 + 0.02 * 128 * nfree(out) if out.shape[0] == 128 else 4000.0)
            if q == "pool":
                if len(pool_dmas) >= MAX_SWDGE_INFLIGHT:
                    op.deps.append(pool_dmas[-MAX_SWDGE_INFLIGHT])
                pool_dmas.append(op)

        A = S.A
        R = S.R

        dma("pool", V(cm, 0, [(128, 3), (1, 128)]), cmat.rearrange("k p c -> p k c"), [], [A("cm")], "c0")
        dma("pool", V(bm, 0, [(1, 512)]), bmask, [], [A("bm")], "c1")
        dma("sp", V(ng, 0, [(1, 8)]), ngd, [], [A("ng")], "c2")
        dma("sp", V(gc, 0, [(1, 4)]), gcd, [], [A("gc")], "c3")
        ts("dve", V(gc, 0, [(2, 2)]), V(gc, 0, [(2, 2)]), 0.125, None, ALU.mult, None, [A("gc")], [A("gc")])
        IDENT = V(cm, 0, [(1, 128)])
        PERM = V(cm, 128, [(1, 128)])
        BD = V(cm, 256, [(1, 128)])

        wstate = {"n": 0}

        def load_w(src_ap_fn, nchunk=8, ncol=128):
            s = wstate["n"] % NSLOT
            wstate["n"] += 1
            dma("pool", V(Wr, s * 1024, [(ncol, nchunk), (1, ncol)]), src_ap_fn, [], [A("W", s)], "w%d" % s)
            return s

        w_in_v = w_in.rearrange("(c p) e -> p c e", p=128)

        def w_in_cols(col0, ncol=128):
            return w_in_v[:, :, col0:col0 + ncol]

        def hT_tiles(t0, t1_):
            return [A("hT", i) for i in range(t0 // 128, (t1_ - 1) // 128 + 1)]

        acc_rr = {"n": 0}
        aux_rr = {"n": 0}
        sc_rr = {"n": 0}
        st_rr = {"n": 0}
        pt_rr = {"n": 0}

        banks = [psA[0], psA[1], psX[0], psX[1], psS[0], psS[1], psO0, psT]

        def next_bank():
            vb = VB()
            S.vbs_all.append(vb)
            return vb, vb.atom

        next_acc = next_bank
        next_aux = next_bank
        next_st = next_bank

        def proj_fm(slot, t0, N):
            ps, pa = next_acc()
            for c in range(8):
                mm(V(ps, 0, [(1, N)]), V(Wr, slot * 1024 + c * 128, [(1, 128)]),
                   V(hT, c * W + t0, [(1, N)]), c == 0, c == 7,
                   [A("W", slot)] + hT_tiles(t0, t0 + N), [pa])
            return ps, pa

        def rsqrt_chain(ps, pa, N, sl):
            o = sl * 512
            act(V(sq, o, [(1, N)]), V(ps, 0, [(1, N)]), AF.Square, [pa], [A("sq", sl)])
            px, pxa = next_aux()
            mm(V(px, 0, [(1, N)]), BD, V(sq, o, [(1, N)]), True, True, [A("cm"), A("sq", sl)], [pxa])
            act(V(rs, o, [(1, N)]), V(px, 0, [(1, N)]), AF.Ln, [pxa], [A("rs", sl)], scale=1.0, bias=EPS)
            act(V(rs, o, [(1, N)]), V(rs, o, [(1, N)]), AF.Exp, [A("rs", sl)], [A("rs", sl)], scale=-0.5)

        def qk_tile(slot, t0, N, gcol, dst_off, d, sub, n_off, cidx):
            ps, pa = proj_fm(slot, t0, N)
            sl = sc_rr["n"] % NSC
            sc_rr["n"] += 1
            o = sl * 512
            rsqrt_chain(ps, pa, N, sl)
            if d == 0:
                c0 = dst_off + (t0 - n_off)
                stt("dve", V(AR, c0, [(1, N)]), V(ps, 0, [(1, N)]), V(gc, gcol, [(1, 1)]), V(rs, o, [(1, N)]),
                    ALU.mult, ALU.mult, [pa, A("gc"), A("rs", sl)], R("AR", c0, c0 + N))
                return
            stt("dve", V(qn, o, [(1, N)]), V(ps, 0, [(1, N)]), V(gc, gcol, [(1, 1)]), V(rs, o, [(1, N)]),
                ALU.mult, ALU.mult, [pa, A("gc"), A("rs", sl)], [A("qn", sl)])
            px, pxa = next_aux()
            mm(V(px, 0, [(1, N)]), PERM, V(qn, o, [(1, N)]), True, True, [A("cm"), A("qn", sl)], [pxa])
            tt("pool", V(t1, o, [(1, N)]), V(qn, o, [(1, N)]), V(RT, t0, [(1, N)]), ALU.mult,
               [A("qn", sl)] + R("RT", t0, t0 + N), [A("t1", sl)])
            tt("dve", V(t2, o, [(1, N)]), V(px, 0, [(1, N)]), V(RT, W + t0, [(1, N)]), ALU.mult,
               [pxa] + R("RT", W + t0, W + t0 + N), [A("t2", sl)])
            n0 = t0 // d - n_off
            cnt = N // d
            wr = []
            for r in range(d):
                wr += R("AR", dst_off + r * sub + n0, dst_off + r * sub + n0 + cnt)
            tt("pool" if (ROPE_ADD_POOL and (sc_rr["n"] % ROPE_ADD_POOL == 0)) else "dve", V(AR, dst_off + n0, [(sub, d), (1, cnt)]),
               V(t1, o, [(1, d), (d, cnt)]), V(t2, o, [(1, d), (d, cnt)]), ALU.add,
               [A("t1", sl), A("t2", sl)], wr)

        QT, KT, VV, SG = 0, 2048, 6144, 14336

        def v_blocks(slot, nblk, tok_fn, d, vf_col):
            cp("dve", V(AR, VV + 64, [(256, nblk), (64, 2), (1, 64)]),
               V(vfl, vf_col, [(1, nblk), (0, 2), (0, 64)]), [A("vfl")], [A("VFLG")])
            for b0 in range(0, nblk, 4):
                nb = min(4, nblk - b0)
                ps, pa = next_acc()
                for jj in range(nb):
                    tk = tok_fn(b0 + jj)
                    for c in range(8):
                        mm(V(ps, jj * 128, [(1, 128)]), V(hT, c * W + tk, [(d, 128)]),
                           V(Wr, slot * 1024 + c * 128, [(1, 128)]), c == 0, c == 7,
                           [A("W", slot)] + hT_tiles(tk, tk + 127 * d + 1), [pa])
                c0 = VV + b0 * 256
                wr = R("AR", c0, c0 + nb * 256)
                cp("dve", V(AR, c0, [(256, nb), (192, 2), (1, 64)]), V(ps, 0, [(128, nb), (64, 2), (1, 64)]),
                   [pa], wr)

        def gate_tiles(slot):
            for i in range(4):
                t0 = Q0 + 512 * i
                ps, pa = proj_fm(slot, t0, 512)
                sl = sc_rr["n"] % NSC
                sc_rr["n"] += 1
                o = sl * 512
                act(V(t1, o, [(1, 512)]), V(ps, 0, [(1, 512)]), AF.Exp, [pa], [A("t1", sl)], scale=-1.0)
                act(V(t1, o, [(1, 512)]), V(t1, o, [(1, 512)]), AF.Ln, [A("t1", sl)], [A("t1", sl)], bias=1.0)
                act(V(t1, o, [(1, 512)]), V(t1, o, [(1, 512)]), AF.Exp, [A("t1", sl)], [A("t1", sl)], scale=-1.0)
                c0 = SG + 512 * i
                tt("dve", V(AR, c0, [(1, 512)]), V(ps, 0, [(1, 512)]), V(t1, o, [(1, 512)]), ALU.mult,
                   [pa, A("t1", sl)], R("AR", c0, c0 + 512))

        mul_rr = {"n": 0}
        nat_rr = {"n": 0}

        def mul_eng():
            return "dve"

        OUTER = list(range(0, 8)) + list(range(24, 32))
        INNER = list(range(8, 24))

        def phase_a(ch, tiles):
            for ti in tiles:
                xs = ti % 4
                dma("sp", V(XA, xs * 1024, [(1, 1024)]), xw[ch, ti * 128:(ti + 1) * 128, :], [], [A("XA", xs)],
                    "x%d" % xs)
                ss = ti % 2
                act(V(xn, ss * 1024, [(1, 1024)]), V(XA, xs * 1024, [(1, 1024)]), AF.Square, [A("XA", xs)],
                    [A("xn", ss), A("st", ss)], accum=V(st_, ss, [(1, 1)]))
                act(V(st_, 2 + ss, [(1, 1)]), V(st_, ss, [(1, 1)]), AF.Ln, [A("st", ss)], [A("st2", ss)],
                    scale=1.0 / D, bias=EPS)
                act(V(st_, 4 + ss, [(1, 1)]), V(st_, 2 + ss, [(1, 1)]), AF.Exp, [A("st2", ss)], [A("st3", ss)],
                    scale=-0.5)
                ts("dve", V(xn, ss * 1024, [(1, 1024)]), V(XA, xs * 1024, [(1, 1024)]), V(st_, 4 + ss, [(1, 1)]),
                   None, ALU.mult, None, [A("XA", xs), A("st3", ss)], [A("xn", ss)])
                vbT, vbTa = next_bank()
                for c in range(8):
                    tr(VT(vbT, c * 128, (c + 1) * 128), V(xn, ss * 1024 + c * 128, [(1, 128)]), IDENT,
                       [A("xn", ss), A("cm")], [vbTa])
                tt("dve", V(hT, ti * 128, [(W, 8), (1, 128)]), VT8(vbT),
                   V(ng, 0, [(1, 8), (0, 128)]), ALU.mult, [vbTa, A("ng")], [A("hT", ti)])

        for ch in range(NCH):
            if ch == 0:
                dma("sp", V(vfl, 0, [(1, NVF)]), vfld[ch], [], [A("vfl")], "vf")
                phase_a(ch, INNER)
                phase_a(ch, OUTER)
            else:
                phase_a(ch, INNER)

            dma("pool", V(RT, 0, [(1, W)]), ropec[ch], [], R("RT", 0, W), "rc")
            dma("pool", V(RT, W, [(1, W)]), ropes[ch], [], R("RT", W, 2 * W), "rs")
            for jp in range(4):
                cb = jp * 128
                if ch == 0:
                    tv = (1, 0, 0)
                elif ch == 1:
                    tv = (0, 0, 1)
                else:
                    tv = (2, 0, 3)
                s_q = load_w(w_in_cols(QA + cb))
                s_k = load_w(w_in_cols(KA + cb))
                s_v = load_w(w_in_cols(VA + cb))
                s_g = load_w(w_in_cols(GA + cb))
                for i in range(4):
                    qk_tile(s_q, Q0 + 512 * i, 512, 0, QT, 0, 0, Q0, 0)
                for i in range(5):
                    qk_tile(s_k, 768 + 512 * i, 512, 1, KT, 0, 0, 0, 0)
                v_blocks(s_v, 20, lambda b: 768 + 128 * b, 1, VF_OFF["na"])
                gate_tiles(s_g)
                def load_tab(var, h2, dst):
                    xs = nat_rr["n"] % 4
                    nat_rr["n"] += 1
                    dma("sp", V(XA, xs * 1024, [(1, 960)]), natab[jp, var, :, h2 * 960:(h2 + 1) * 960],
                        [], [A("XA", xs)], "x%d" % xs)
                    act(V(NAT, dst * 960, [(1, 960)]), V(XA, xs * 1024, [(1, 960)]), AF.Exp, [A("XA", xs)],
                        [A("NAT", dst)])
                for h2 in range(2):
                    load_tab(tv[1], h2, h2)
                    load_tab(tv[0], h2, 2 + h2)
                for hh in range(2):
                    pq = hh * 64
                    pd = (1 - hh) * 64
                    for rg in range(8):
                        if rg == 7:
                            load_tab(tv[2], hh, 2 + hh)
                        tslot = (2 + hh) if rg in (0, 7) else hh
                        if rg % 2 == 0:
                            psO, psOa = next_bank()
                        oo = 256 * (rg % 2)
                        for pp in range(3):
                            stp, sta = next_st()
                            for slot_i, kb in ((0, 2 * pp + 1), (1, 2 * pp)):
                                j = 2 * rg + kb
                                kc = KT + 768 + 128 * j
                                mm(V(stp, slot_i * 256, [(1, 256)]), V(AR, kc, [(1, 128)], pq, 64),
                                   V(AR, QT + 256 * rg, [(1, 256)], pq, 64), True, True,
                                   R("AR", kc, kc + 128) + R("AR", QT + 256 * rg, QT + 256 * rg + 256), [sta])
                            pts = pt_rr["n"] % 4
                            pt_rr["n"] += 1
                            act(V(PT, pts * 512, [(1, 512)]), V(stp, 0, [(1, 512)]), AF.Exp, [sta], [A("PT", pts)])
                            tc0 = tslot * 960 + (9 - 4 * pp) * 64
                            tt(mul_eng(), V(PT, pts * 512, [(256, 2), (1, 256)]),
                               V(PT, pts * 512, [(256, 2), (1, 256)]), V(NAT, tc0, [(128, 2), (1, 256)]), ALU.mult,
                               [A("PT", pts), A("NAT", tslot)], [A("PT", pts)])
                            for slot_i, kb in ((0, 2 * pp + 1), (1, 2 * pp)):
                                j = 2 * rg + kb
                                vc = VV + j * 256 + hh * 128
                                first = (pp == 0 and slot_i == 0)
                                last = (pp == 2 and slot_i == 1)
                                mm(V(psO, oo, [(1, 256)]), V(AR, vc, [(1, 128)]),
                                   V(PT, pts * 512 + slot_i * 256, [(1, 256)]), first, last,
                                   R("AR", vc, vc + 128) + [A("PT", pts), A("VFLG")], [psOa])
                        if rg % 2 == 0:
                            continue
                        sl = sc_rr["n"] % NSC
                        sc_rr["n"] += 1
                        o = sl * 512
                        g0 = 256 * (rg - 1)
                        act(V(rs, o, [(1, 512)], pq, 64), V(psO, 0, [(1, 512)], pd, 64), AF.Ln, [psOa],
                            [A("rs", sl)])
                        act(V(rs, o, [(1, 512)], pq, 64), V(rs, o, [(1, 512)], pq, 64), AF.Exp, [A("rs", sl)],
                            [A("rs", sl)], scale=-1.0)
                        tt("dve", V(t1, o, [(1, 512)], pq, 64), V(psO, 0, [(1, 512)], pq, 64),
                           V(rs, o, [(1, 512)], pq, 64), ALU.mult, [psOa, A("rs", sl)], [A("t1", sl)])
                        oc = (0 * 4 + jp) * NQ + g0
                        tt(FIN_ENG, V(og, oc, [(1, 512)], pq, 64), V(t1, o, [(1, 512)], pq, 64),
                           V(AR, SG + g0, [(1, 512)], pq, 64), ALU.mult,
                           [A("t1", sl)] + R("AR", SG + g0, SG + g0 + 512),
                           [A("og", 0, jp, rg // 2)])

                s_g = load_w(w_in_cols(GB + cb))
                gate_tiles(s_g)
                for gi, (d, qc, kc_, vc_) in enumerate(DIL):
                    sub = W // d
                    qsub = NQ // d
                    nj = 16 // d + 1
                    s_q = load_w(w_in_cols(qc + cb))
                    s_k = load_w(w_in_cols(kc_ + cb))
                    s_v = load_w(w_in_cols(vc_ + cb))
                    for i in range(4):
                        qk_tile(s_q, Q0 + 512 * i, 512, 2, QT, d, qsub, Q0 // d, 0)
                    if d == 1:
                        ktiles = [(960 + 512 * i, 512) for i in range(4)] + [(3008, 128)]
                    elif d == 4:
                        ktiles = [(768 + 512 * i, 512) for i in range(5)]
                    else:
                        ktiles = [(512 * i, 512) for i in range(8)]
                    for (t0, N) in ktiles:
                        qk_tile(s_k, t0, N, 3, KT, d, sub, 0, 0)
                    kbase = Q0 // d - 64

                    def tokfn(b, d=d, nj=nj, kbase=kbase):
                        r, j = divmod(b, nj)
                        return (kbase + 128 * j) * d + r
                    v_blocks(s_v, d * nj, tokfn, d, VF_OFF[d])
                    nb_r = 16 // d
                    for hh in range(2):
                        pq = hh * 64
                        for qp in range(8):
                            stp, sta = next_st()
                            if qp % 2 == 0:
                                psO, psOa = next_bank()
                                info4 = []
                            po = 256 * (qp % 2)
                            info = []
                            for qi in range(2):
                                qb = 2 * qp + qi
                                r, b = divmod(qb, nb_r)
                                info.append((r, b))
                                qcol = QT + r * qsub + 128 * b
                                for half in range(2):
                                    j = b + half
                                    kcol = KT + r * sub + kbase + 128 * j
                                    mm(V(stp, qi * 256 + half * 128, [(1, 128)]), V(AR, kcol, [(1, 128)], pq, 64),
                                       V(AR, qcol, [(1, 128)], pq, 64), True, True,
                                       R("AR", kcol, kcol + 128) + R("AR", qcol, qcol + 128), [sta])
                            pts = pt_rr["n"] % 4
                            pt_rr["n"] += 1
                            act(V(PT, pts * 512, [(1, 512)]), V(stp, 0, [(1, 512)]), AF.Exp, [sta], [A("PT", pts)])
                            tt(mul_eng(), V(PT, pts * 512, [(1, 512)]), V(PT, pts * 512, [(1, 512)]),
                               V(bm, 0, [(1, 512)]), ALU.mult, [A("PT", pts), A("bm")], [A("PT", pts)])
                            for qi in range(2):
                                r, b = info[qi]
                                for half in range(2):
                                    kbi = r * nj + b + half
                                    vc = VV + kbi * 256 + hh * 128
                                    mm(V(psO, po + qi * 128, [(1, 128)]), V(AR, vc, [(1, 128)]),
                                       V(PT, pts * 512 + qi * 256 + half * 128, [(1, 128)]), half == 0, half == 1,
                                       R("AR", vc, vc + 128) + [A("PT", pts), A("VFLG")], [psOa])
                            info4 += info
                            if qp % 2 == 0:
                                continue
                            offs = [r_ + 128 * b_ * d for (r_, b_) in info4]
                            st4 = offs[1] - offs[0]
                            assert offs[2] - offs[1] == st4 and offs[3] - offs[2] == st4
                            accv = V(XA, hh * 2048 + offs[0], [(st4, 4), (d, 128)])
                            acca = [A("XA", 2 * hh), A("XA", 2 * hh + 1)]
                            if gi == 0:
                                cp("dve", accv, V(psO, 0, [(128, 4), (1, 128)]), [psOa], acca)
                            else:
                                tt("dve", accv, V(psO, 0, [(128, 4), (1, 128)]), accv, ALU.add,
                                   [psOa] + acca, acca)
                for hh in range(2):
                    pq = hh * 64
                    pd = (1 - hh) * 64
                    for i in range(4):
                        sl = sc_rr["n"] % NSC
                        sc_rr["n"] += 1
                        o = sl * 512
                        acca = [A("XA", 2 * hh), A("XA", 2 * hh + 1)]
                        act(V(rs, o, [(1, 512)], pq, 64), V(XA, hh * 2048 + 512 * i, [(1, 512)], pd, 64), AF.Ln,
                            acca, [A("rs", sl)])
                        act(V(rs, o, [(1, 512)], pq, 64), V(rs, o, [(1, 512)], pq, 64), AF.Exp, [A("rs", sl)],
                            [A("rs", sl)], scale=-1.0)
                        tt("dve", V(t1, o, [(1, 512)], pq, 64), V(XA, hh * 2048 + 512 * i, [(1, 512)], pq, 64),
                           V(rs, o, [(1, 512)], pq, 64), ALU.mult, acca + [A("rs", sl)], [A("t1", sl)])
                        oc = (1 * 4 + jp) * NQ + 512 * i
                        tt(FIN_ENG, V(og, oc, [(1, 512)], pq, 64), V(t1, o, [(1, 512)], pq, 64),
                           V(AR, SG + 512 * i, [(1, 512)], pq, 64), ALU.mult,
                           [A("t1", sl)] + R("AR", SG + 512 * i, SG + 512 * i + 512), [A("og", 1, jp, i)])

            if ch + 1 < NCH:
                dma("sp", V(vfl, 0, [(1, NVF)]), vfld[ch + 1], [], [A("vfl")], "vf")
                phase_a(ch + 1, OUTER)
            dma("pool", V(RT, 0, [(1024, 8), (1, 1024)]), wout.rearrange("(c p) e -> p c e", p=128), [],
                R("RT", 0, 8192), "wo")
            for ec in range(8):
                s_ma = load_w(w_in_cols(MA + ec * 128))
                s_mb = load_w(w_in_cols(MB + ec * 128))
                s_ba = load_w(wba.rearrange("(c p) e -> p c e", p=128)[:, :, ec * 128:(ec + 1) * 128], nchunk=4)
                s_bb = load_w(wbb.rearrange("(c p) e -> p c e", p=128)[:, :, ec * 128:(ec + 1) * 128], nchunk=4)
                for tq in range(4):
                    t0 = Q0 + 512 * tq
                    pma, pmaa = proj_fm(s_ma, t0, 512)
                    pmb, pmba = proj_fm(s_mb, t0, 512)
                    pba, pbaa = next_aux()
                    pbb, pbba = next_aux()
                    for ab, (pb_, pba_, sslot) in enumerate(((pba, pbaa, s_ba), (pbb, pbba, s_bb))):
                        for jp in range(4):
                            mm(V(pb_, 0, [(1, 512)]), V(Wr, sslot * 1024 + jp * 128, [(1, 128)]),
                               V(og, (ab * 4 + jp) * NQ + 512 * tq, [(1, 512)]), jp == 0, jp == 3,
                               [A("W", sslot), A("og", ab, jp, tq)], [pba_])
                    sl = sc_rr["n"] % NSC
                    sc_rr["n"] += 1
                    o = sl * 512
                    for (pm, pma_, pb_, pba_, tb, tbn) in ((pma, pmaa, pba, pbaa, t1, "t1"),
                                                          (pmb, pmba, pbb, pbba, t2, "t2")):
                        act(V(tb, o, [(1, 512)]), V(pm, 0, [(1, 512)]), AF.Exp, [pma_], [A(tbn, sl)], scale=-1.0)
                        act(V(tb, o, [(1, 512)]), V(tb, o, [(1, 512)]), AF.Ln, [A(tbn, sl)], [A(tbn, sl)], bias=1.0)
                        act(V(tb, o, [(1, 512)]), V(tb, o, [(1, 512)]), AF.Exp, [A(tbn, sl)], [A(tbn, sl)],
                            scale=-1.0)
                        tt("dve", V(tb, o, [(1, 512)]), V(pb_, 0, [(1, 512)]), V(tb, o, [(1, 512)]), ALU.mult,
                           [pba_, A(tbn, sl)], [A(tbn, sl)])
                    mc = ec * NQ + 512 * tq
                    tt("pool" if MERGE_ADD_POOL else "dve", V(AR, mc, [(1, 512)]), V(t1, o, [(1, 512)]), V(t2, o, [(1, 512)]), ALU.add,
                       [A("t1", sl), A("t2", sl)], R("AR", mc, mc + 512) + [A("VFLG")])
            for ti in range(16):
                xs = ti % 4
                tok = Q0 + 128 * ti
                dma("sp", V(XA, xs * 1024, [(1, 1024)]), xw[ch, tok:tok + 128, :], [], [A("XA", xs)], "x%d" % xs)
                for half in range(2):
                    stp, sta = next_st()
                    for ec in range(8):
                        mc = ec * NQ + 128 * ti
                        mm(V(stp, 0, [(1, 512)]), V(AR, mc, [(1, 128)]),
                           V(RT, ec * 1024 + half * 512, [(1, 512)]), ec == 0, ec == 7,
                           R("AR", mc, mc + 128) + R("RT", ec * 1024 + half * 512, ec * 1024 + half * 512 + 512)
                           + [A("VFLG")], [sta])
                    tt("dve", V(XA, xs * 1024 + half * 512, [(1, 512)]), V(stp, 0, [(1, 512)]),
                       V(XA, xs * 1024 + half * 512, [(1, 512)]), ALU.add, [sta, A("XA", xs)], [A("XA", xs)])
                dma("sp", yo[ch, 128 * ti:128 * (ti + 1), :], V(XA, xs * 1024, [(1, 1024)]), [A("XA", xs)], [],
                    "x%d" % xs)

        if DO_SCHED:
            S.schedule(window=WINDOW, nphys=NPHYS)
        else:
            S.est_ns = 0
        print('[sched] est_us=%.1f ops=%s' % (S.est_ns / 1e3, {e: len(S.ops[e]) for e in S.ENGS}), flush=True)
        pos = {}
        for e in S.ENGS:
            for i_, op in enumerate(S.ops[e]):
                pos[id(op)] = i_
        waits_of = {}
        for e in S.ENGS:
            seen = {}
            for op in S.ops[e]:
                grp = {}
                for p in op.deps:
                    kk = p.eng if p.dma is None else ("d", p.dma)
                    q = grp.get(kk)
                    if q is None or pos[id(p)] > pos[id(q)]:
                        grp[kk] = p
                wl = []
                for kk, p in grp.items():
                    if seen.get(kk, -1) >= pos[id(p)]:
                        continue
                    seen[kk] = pos[id(p)]
                    p.needed = True
                    wl.append(p)
                waits_of[id(op)] = wl
        sems = {}
        for e in S.ENGS:
            sems[e] = es.enter_context(nc.semaphore("sem_" + e))
        dsems = {}
        for k in S.dma_keys:
            dsems[k] = es.enter_context(nc.semaphore("dsem_" + k))
        dcount = {k: 0 for k in S.dma_keys}
        nsig = {}
        for e in S.ENGS:
            cnt = 0
            for op in S.ops[e]:
                if op.dma is not None:
                    continue
                if op.needed:
                    cnt += 1
                    op.sig = (sems[e], cnt)
            nsig[e] = cnt
        for e in S.ENGS:
            for op in S.ops[e]:
                if op.dma is not None:
                    dcount[op.dma] += 16
                    op.sig = (dsems[op.dma], dcount[op.dma])
        print('[sync] signalling ops per engine:', nsig, flush=True)

        def run_engine(ename, eh):
            for op in S.ops[ename]:
                for p in waits_of[id(op)]:
                    eh.wait_ge(p.sig[0], p.sig[1])
                ins = op.fn(eh)
                if op.dma is not None:
                    ins.then_inc(op.sig[0], 16)
                elif op.needed:
                    ins.then_inc(op.sig[0], 1)
            if ename == "sp":
                for k, v in dcount.items():
                    if v > 0:
                        eh.wait_ge(dsems[k], v)

        with nc.Block() as block:
            @block.tensor
            def _(e):
                run_engine("pe", e)

            @block.scalar
            def _(e):
                run_engine("act", e)

            @block.vector
            def _(e):
                run_engine("dve", e)

            @block.gpsimd
            def _(e):
                run_engine("pool", e)

            @block.sync
            def _(e):
                run_engine("sp", e)
    return nc


def _na_tables(rel_bias):
    rb = np.asarray(rel_bias, np.float32)
    p = np.arange(128)
    half = p // 64
    ck = p % 64
    s = np.arange(15)
    cq = np.arange(64)
    drow = 7 - s[None, :] + half[:, None]
    cstart = np.clip(cq - 8, 0, 48)
    colok = (ck[:, None] >= cstart[None, :]) & (ck[:, None] < cstart[None, :] + 16)
    dcol = np.clip(ck[:, None] - cq[None, :], -15, 15) + 15
    out = np.full((4, 4, 128, 2, 15, 64), -30000.0, np.float32)
    for var in range(2):
        if var == 0:
            rowok = (drow >= -4) & (drow <= 3)
        else:
            rowok = (drow >= -7) & (drow <= 7)
        ok = rowok[:, :, None] & colok[:, None, :]
        ridx = np.clip(drow + 7, 0, 14)
        for h in range(8):
            vals = rb[h][ridx[:, :, None], dcol[:, None, :]]
            out[h // 2, var, :, h % 2] = np.where(ok, vals, np.float32(-30000.0))
    return out.reshape(4, 4, 128, 1920)


def _rope_tables(pos):
    inv = (10000.0 ** (-np.arange(0, 64, 2, dtype=np.float32) / np.float32(64))).astype(np.float32)
    ang = (pos.astype(np.float32)[None, :] * inv[:, None]).astype(np.float32)
    c = np.cos(ang.astype(np.float64)).astype(np.float32)
    s = np.sin(ang.astype(np.float64)).astype(np.float32)
    C = np.tile(c, (4, 1))
    Sg = np.concatenate([-s, s, -s, s], axis=0)
    return C, Sg


def _valid_flags(valid):
    out = np.zeros((128, NVF), np.float32)
    i = np.arange(128)
    for j in range(20):
        out[:, VF_OFF["na"] + j] = valid[768 + 128 * j + i]
    for d in (1, 4, 16):
        nj = 16 // d + 1
        kbase = Q0 // d - 64
        for r in range(d):
            for j in range(nj):
                out[:, VF_OFF[d] + r * nj + j] = valid[(kbase + 128 * j + i) * d + r]
    return out


_CACHE = {}


def kernel(x_prompt, x_sample, norm_gain, w_in, qn_a, kn_a, rel_bias_a, qn_b, kn_b, w_branch_a, w_branch_b, w_out):
    x_prompt = np.asarray(x_prompt, np.float32)
    x_sample = np.asarray(x_sample, np.float32)
    if "nc" not in _CACHE:
        _CACHE["nc"] = build_program()
    nc = _CACHE["nc"]

    w_in2 = np.ascontiguousarray(np.asarray(w_in, np.float32)[0])
    wba = np.ascontiguousarray(np.asarray(w_branch_a, np.float32)[0])
    wbb = np.ascontiguousarray(np.asarray(w_branch_b, np.float32)[0])
    wo = np.ascontiguousarray(np.asarray(w_out, np.float32)[0])
    ng = np.ascontiguousarray(np.asarray(norm_gain, np.float32)[0].reshape(8, 128).T)
    gcols = np.stack([np.tile(np.asarray(a, np.float32)[0], 2) for a in (qn_a, qn_b, kn_a, kn_b)], axis=1)
    gcols = np.ascontiguousarray(gcols[:, [0, 2, 1, 3]])
    nat = _na_tables(np.asarray(rel_bias_a, np.float32)[0])
    ident = np.eye(128, dtype=np.float32)
    perm = np.zeros((128, 128), np.float32)
    for m in range(128):
        perm[m, m + 32 if (m % 64) < 32 else m - 32] = 1.0
    bd = np.zeros((128, 128), np.float32)
    bd[:64, :64] = 1.0 / 64
    bd[64:, 64:] = 1.0 / 64
    cmat = np.stack([ident, perm, bd])
    kk = np.arange(128)[:, None]
    qq = np.arange(128)[None, :]
    m1 = np.concatenate([(qq <= kk), (qq >= kk)], axis=1).astype(np.float32)
    bmask = np.ascontiguousarray(np.concatenate([m1, m1], axis=1))

    in_maps = []
    for c in range(NCORES):
        xw = np.zeros((NCH, W, D), np.float32)
        vfl = np.zeros((NCH, 128, NVF), np.float32)
        rc = np.zeros((NCH, 128, W), np.float32)
        rsn = np.zeros((NCH, 128, W), np.float32)
        for ch, (src, L, q0) in enumerate(((x_prompt[c], 4096, 0), (x_prompt[c], 4096, 2048),
                                           (x_sample[0], 16384, 2048 * c))):
            lo = q0 - Q0
            pos = np.arange(lo, lo + W)
            valid = ((pos >= 0) & (pos < L))
            a = max(lo, 0)
            b = min(lo + W, L)
            xw[ch, a - lo:b - lo] = src[a:b]
            vfl[ch] = _valid_flags(valid.astype(np.float32))
            C, Sg = _rope_tables(np.clip(pos, 0, L - 1))
            rc[ch] = C
            rsn[ch] = Sg
        natc = nat.copy()
        natc[:, 2] = nat[:, 1] if c == 0 else nat[:, 0]
        natc[:, 3] = nat[:, 1] if c == NCORES - 1 else nat[:, 0]
        in_maps.append({"xw": xw, "w_in": w_in2, "wba": wba, "wbb": wbb, "wout": wo, "ng": ng, "gcols": gcols,
                        "ropec": rc, "ropes": rsn, "natab": natc, "vfl": vfl, "cmat": cmat, "bmask": bmask})

    res = run_bass_kernel_spmd(nc, in_maps, core_ids=list(range(NCORES)))
    y_prompt = np.empty((8, 4096, D), np.float32)
    y_sample = np.empty((1, 16384, D), np.float32)
    for c in range(NCORES):
        yo = res.results[c]["yo"]
        y_prompt[c, 0:2048] = yo[0]
        y_prompt[c, 2048:4096] = yo[1]
        y_sample[0, 2048 * c:2048 * (c + 1)] = yo[2]
    return (y_prompt, y_sample)
```
